# Optimizing a Trainium2 kernel written in Bass

```python
import jax, jax.numpy as jnp
from jax import lax
import numpy as np

D_MODEL = 1024
BATCH = 4
SEQ = 8192
DEPTH = 1

MLSTM_HEADS = 4
MLSTM_DV = D_MODEL // MLSTM_HEADS
MLSTM_DQK = MLSTM_DV // 2
MLSTM_CHUNK = 64
CONV_K = 4
NSA_DH = 64
NSA_HEADS = (D_MODEL // 2) // NSA_DH
NSA_KV_GROUPS = 2
NSA_HPG = NSA_HEADS // NSA_KV_GROUPS
CMP_BLOCK = 32
CMP_STRIDE = 16
CMP_HIDDEN = 256
SLC_BLOCK = 64
SLC_TOPN = 16
WINDOW = 512
NSA_QBLOCK = 128
FFN_HIDDEN = ((8 * D_MODEL + 3 * 256 - 1) // (3 * 256)) * 256
RMS_EPS = 1e-6
NEG_INF = -1e30
SEL_BIG = 1e9

M_QK = MLSTM_HEADS * MLSTM_DQK
M_V = MLSTM_HEADS * MLSTM_DV
N_Q = NSA_HEADS * NSA_DH
N_KV = NSA_KV_GROUPS * NSA_DH
IN_SPLITS = (2 * M_QK, M_V, M_V, MLSTM_HEADS, MLSTM_HEADS, N_Q, 6 * N_KV, 3 * NSA_HEADS, D_MODEL, D_MODEL)
IN_WIDTH = sum(IN_SPLITS)

kernel_name = 'hybrid_mlstm_nsa_block'


def rms_norm(x, g):
    xf = x.astype(jnp.float32)
    y = xf * lax.rsqrt(jnp.mean(xf * xf, axis=-1, keepdims=True) + RMS_EPS)
    return (y * g.astype(jnp.float32)).astype(x.dtype)


def masked_softmax(s, mask):
    s = jnp.where(mask, s.astype(jnp.float32), NEG_INF)
    return jnp.where(mask, jax.nn.softmax(s, axis=-1), 0.0)


def causal_conv(x, w, b):
    c = x.shape[-1]
    y = lax.conv_general_dilated(x, w[:, None, :], window_strides=(1,), padding=[(CONV_K - 1, 0)],
                                 dimension_numbers=('NWC', 'WIO', 'NWC'), feature_group_count=c)
    return y + b


def mlstm_chunkwise(q, k, v, log_i, log_f):
    B, H, S, DK = q.shape
    DV = v.shape[-1]
    L = MLSTM_CHUNK
    nc = S // L

    def chunks(a):
        return jnp.moveaxis(a.reshape(B, H, nc, L, *a.shape[3:]), 2, 0)

    causal = jnp.tril(jnp.ones((L, L), dtype=bool))

    def step(carry, inp):
        C, n, m = carry
        qc, kc, vc, ic, fc = inp
        b = jnp.cumsum(fc, axis=-1)
        d_intra = jnp.where(causal, b[..., :, None] - b[..., None, :] + ic[..., None, :], -jnp.inf)
        d_inter = b + m[..., None]
        m_t = jnp.maximum(d_inter, jnp.max(d_intra, axis=-1))
        w_intra = jnp.exp(d_intra - m_t[..., None])
        w_inter = jnp.exp(d_inter - m_t)
        s = jnp.einsum('bhld,bhsd->bhls', qc, kc) * w_intra
        num = jnp.einsum('bhls,bhsv->bhlv', s, vc) + w_inter[..., None] * jnp.einsum('bhld,bhvd->bhlv', qc, C)
        nq = jnp.sum(s, axis=-1) + w_inter * jnp.einsum('bhld,bhd->bhl', qc, n)
        h = num / jnp.maximum(jnp.abs(nq), jnp.exp(-m_t))[..., None]
        b_last = b[..., -1]
        d_state = b_last[..., None] - b + ic
        m_new = jnp.maximum(b_last + m, jnp.max(d_state, axis=-1))
        w_state = jnp.exp(d_state - m_new[..., None])
        decay = jnp.exp(b_last + m - m_new)
        C_new = decay[..., None, None] * C + jnp.einsum('bhsv,bhsd->bhvd', vc * w_state[..., None], kc)
        n_new = decay[..., None] * n + jnp.einsum('bhs,bhsd->bhd', w_state, kc)
        return (C_new, n_new, m_new), h

    init = (jnp.zeros((B, H, DV, DK), jnp.float32), jnp.zeros((B, H, DK), jnp.float32), jnp.zeros((B, H), jnp.float32))
    _, h = lax.scan(step, init, (chunks(q), chunks(k), chunks(v), chunks(log_i), chunks(log_f)))
    return jnp.moveaxis(h, 0, 2).reshape(B, H, S, DV)


def mlstm_branch(qk, v, o, i_pre, f_pre, f_bias, conv_w, conv_b, norm_g):
    B, S, _ = v.shape
    qk = jax.nn.silu(causal_conv(qk, conv_w, conv_b))
    q, k = jnp.split(qk, 2, axis=-1)

    def heads(a, d):
        return a.reshape(B, S, MLSTM_HEADS, d).transpose(0, 2, 1, 3).astype(jnp.float32)

    q = heads(q, MLSTM_DQK)
    k = heads(k, MLSTM_DQK) * (MLSTM_DQK ** -0.5)
    vh = heads(v, MLSTM_DV)
    log_i = i_pre.astype(jnp.float32).transpose(0, 2, 1)
    log_f = jax.nn.log_sigmoid((f_pre + f_bias).astype(jnp.float32)).transpose(0, 2, 1)
    h = mlstm_chunkwise(q, k, vh, log_i, log_f).transpose(0, 2, 1, 3)
    h = rms_norm(h, norm_g.reshape(MLSTM_HEADS, MLSTM_DV))
    h = jax.nn.sigmoid(o.astype(jnp.float32)).reshape(B, S, MLSTM_HEADS, MLSTM_DV) * h
    return h.reshape(B, S, M_V).astype(v.dtype)


def compress_blocks(tok, pos_emb, w1, w2):
    B, S, G, dh = tok.shape
    n_cmp = (S - CMP_BLOCK) // CMP_STRIDE + 1
    idx = np.arange(n_cmp)[:, None] * CMP_STRIDE + np.arange(CMP_BLOCK)[None, :]
    blocks = tok[:, idx] + pos_emb[None, None, :, None, :]
    blocks = blocks.transpose(0, 1, 3, 2, 4).reshape(B, n_cmp, G, CMP_BLOCK * dh)
    return jax.nn.silu(blocks @ w1) @ w2


def nsa_branch(q, g_br, kv, cmp_k_pos, cmp_k_w1, cmp_k_w2, cmp_v_pos, cmp_v_w1, cmp_v_w2):
    B, S, _ = q.shape
    G, HPG, dh, Qb = NSA_KV_GROUPS, NSA_HPG, NSA_DH, NSA_QBLOCK
    k_cmp, v_cmp, k_slc, v_slc, k_win, v_win = [a.reshape(B, S, G, dh) for a in jnp.split(kv, 6, axis=-1)]
    kc = compress_blocks(k_cmp, cmp_k_pos, cmp_k_w1, cmp_k_w2)
    vc = compress_blocks(v_cmp, cmp_v_pos, cmp_v_w1, cmp_v_w2)
    n_cmp = kc.shape[1]
    cmp_end = jnp.arange(n_cmp) * CMP_STRIDE + CMP_BLOCK - 1
    n_slc = S // SLC_BLOCK
    n_sel = min(SLC_TOPN, n_slc)
    ratio, span = SLC_BLOCK // CMP_STRIDE, CMP_BLOCK // CMP_STRIDE
    offs = (np.arange(ratio)[:, None] - np.arange(span)[None, :]).reshape(-1)
    map_idx = np.arange(n_slc)[:, None] * ratio + offs[None, :]
    map_valid = (map_idx >= 0) & (map_idx < n_cmp)
    map_idx = np.clip(map_idx, 0, n_cmp - 1)
    ks_blocks = k_slc.reshape(B, n_slc, SLC_BLOCK, G, dh).transpose(0, 3, 1, 2, 4)
    vs_blocks = v_slc.reshape(B, n_slc, SLC_BLOCK, G, dh).transpose(0, 3, 1, 2, 4)
    kw_pad = jnp.pad(k_win, ((0, 0), (WINDOW, 0), (0, 0), (0, 0)))
    vw_pad = jnp.pad(v_win, ((0, 0), (WINDOW, 0), (0, 0), (0, 0)))
    bi = jnp.arange(B)[:, None, None, None]
    gi = jnp.arange(G)[None, None, :, None]
    blk = jnp.arange(n_slc)
    scale = dh ** -0.5

    def block_fn(args):
        qb, gb, q0 = args
        t = q0 + jnp.arange(Qb)
        qg = qb.reshape(B, Qb, G, HPG, dh) * scale
        s = jnp.einsum('bqghd,bcgd->bqghc', qg, kc)
        mask_c = (cmp_end[None, :] <= t[:, None])[None, :, None, None, :]
        p_c = masked_softmax(s, mask_c)
        o_cmp = jnp.einsum('bqghc,bcgd->bqghd', p_c.astype(vc.dtype), vc)
        imp = jnp.sum(p_c, axis=3)
        imp = jnp.sum(jnp.where(map_valid, imp[..., map_idx], 0.0), axis=-1)
        cur = t // SLC_BLOCK
        eligible = blk[None, :] <= cur[:, None]
        forced = (blk[None, :] == 0) | (blk[None, :] == cur[:, None]) | (blk[None, :] == cur[:, None] - 1)
        score = jnp.where(forced[None, :, None, :], SEL_BIG, jnp.where(eligible[None, :, None, :], imp, -SEL_BIG))
        _, idx = lax.top_k(score, n_sel)
        kg = ks_blocks[bi, gi, idx]
        vg = vs_blocks[bi, gi, idx]
        pos = idx[..., None] * SLC_BLOCK + jnp.arange(SLC_BLOCK)
        mask_s = (pos <= t[None, :, None, None, None]).reshape(B, Qb, G, 1, n_sel * SLC_BLOCK)
        s = jnp.einsum('bqghd,bqgnrd->bqghnr', qg, kg).reshape(B, Qb, G, HPG, n_sel * SLC_BLOCK)
        p_s = masked_softmax(s, mask_s).reshape(B, Qb, G, HPG, n_sel, SLC_BLOCK)
        o_slc = jnp.einsum('bqghnr,bqgnrd->bqghd', p_s.astype(vg.dtype), vg)
        kw = lax.dynamic_slice_in_dim(kw_pad, q0, Qb + WINDOW, axis=1)
        vw = lax.dynamic_slice_in_dim(vw_pad, q0, Qb + WINDOW, axis=1)
        kpos = q0 - WINDOW + jnp.arange(Qb + WINDOW)
        mask_w = ((kpos[None, :] <= t[:, None]) & (kpos[None, :] > t[:, None] - WINDOW) & (kpos[None, :] >= 0))[None, :, None, None, :]
        s = jnp.einsum('bqghd,bkgd->bqghk', qg, kw)
        p_w = masked_softmax(s, mask_w)
        o_win = jnp.einsum('bqghk,bkgd->bqghd', p_w.astype(vw.dtype), vw)
        gates = jax.nn.sigmoid(gb.astype(jnp.float32)).reshape(B, Qb, G, HPG, 3).astype(o_cmp.dtype)
        o = gates[..., 0:1] * o_cmp + gates[..., 1:2] * o_slc + gates[..., 2:3] * o_win
        return o.reshape(B, Qb, N_Q)

    n_qb = S // Qb
    q_blocks = q.reshape(B, n_qb, Qb, N_Q).transpose(1, 0, 2, 3)
    g_blocks = g_br.reshape(B, n_qb, Qb, 3 * NSA_HEADS).transpose(1, 0, 2, 3)
    starts = jnp.arange(n_qb, dtype=jnp.int32) * Qb
    out = lax.map(block_fn, (q_blocks, g_blocks, starts))
    return out.transpose(1, 0, 2, 3).reshape(B, S, N_Q)


def setup_inputs(seed: int = 0) -> dict:
    key = jax.random.key(seed)
    ks = jax.random.split(key, 24)

    def nrm(k, shape, scale):
        return jax.random.normal(k, shape, jnp.float32) * scale

    ph = CMP_BLOCK * NSA_DH
    return {
        'x': nrm(ks[0], (BATCH, SEQ, D_MODEL), 1.0),
        'norm1_g': 1.0 + nrm(ks[1], (DEPTH, D_MODEL), 0.02),
        'w_in': nrm(ks[2], (DEPTH, D_MODEL, IN_WIDTH), D_MODEL ** -0.5),
        'b_in': nrm(ks[3], (DEPTH, IN_WIDTH), 0.02),
        'f_bias': jnp.linspace(3.0, 6.0, MLSTM_HEADS, dtype=jnp.float32)[None, :] + nrm(ks[4], (DEPTH, MLSTM_HEADS), 0.1),
        'conv_w': nrm(ks[5], (DEPTH, CONV_K, 2 * M_QK), CONV_K ** -0.5),
        'conv_b': nrm(ks[6], (DEPTH, 2 * M_QK), 0.02),
        'mlstm_norm_g': 1.0 + nrm(ks[7], (DEPTH, M_V), 0.02),
        'cmp_k_pos': nrm(ks[8], (DEPTH, CMP_BLOCK, NSA_DH), 0.02),
        'cmp_k_w1': nrm(ks[9], (DEPTH, ph, CMP_HIDDEN), ph ** -0.5),
        'cmp_k_w2': nrm(ks[10], (DEPTH, CMP_HIDDEN, NSA_DH), CMP_HIDDEN ** -0.5),
        'cmp_v_pos': nrm(ks[11], (DEPTH, CMP_BLOCK, NSA_DH), 0.02),
        'cmp_v_w1': nrm(ks[12], (DEPTH, ph, CMP_HIDDEN), ph ** -0.5),
        'cmp_v_w2': nrm(ks[13], (DEPTH, CMP_HIDDEN, NSA_DH), CMP_HIDDEN ** -0.5),
        'w_branch_a': nrm(ks[14], (DEPTH, M_V, D_MODEL), M_V ** -0.5),
        'w_branch_b': nrm(ks[15], (DEPTH, N_Q, D_MODEL), N_Q ** -0.5),
        'w_out': nrm(ks[16], (DEPTH, D_MODEL, D_MODEL), D_MODEL ** -0.5),
        'norm2_g': 1.0 + nrm(ks[17], (DEPTH, D_MODEL), 0.02),
        'w_ffn_gate': nrm(ks[18], (DEPTH, D_MODEL, FFN_HIDDEN), D_MODEL ** -0.5),
        'w_ffn_up': nrm(ks[19], (DEPTH, D_MODEL, FFN_HIDDEN), D_MODEL ** -0.5),
        'w_ffn_down': nrm(ks[20], (DEPTH, FFN_HIDDEN, D_MODEL), FFN_HIDDEN ** -0.5),
        'norm_f_g': 1.0 + nrm(ks[21], (D_MODEL,), 0.02),
    }


def reference(x, norm1_g, w_in, b_in, f_bias, conv_w, conv_b, mlstm_norm_g, cmp_k_pos, cmp_k_w1, cmp_k_w2,
              cmp_v_pos, cmp_v_w1, cmp_v_w2, w_branch_a, w_branch_b, w_out, norm2_g, w_ffn_gate, w_ffn_up,
              w_ffn_down, norm_f_g):
    split_at = np.cumsum(IN_SPLITS)[:-1].tolist()
    for l in range(DEPTH):
        h = rms_norm(x, norm1_g[l])
        proj = h @ w_in[l] + b_in[l]
        (m_qk, m_v, m_o, m_i, m_f, n_q, n_kv, n_g, gate_a, gate_b) = jnp.split(proj, split_at, axis=-1)
        y_a = mlstm_branch(m_qk, m_v, m_o, m_i, m_f, f_bias[l], conv_w[l], conv_b[l], mlstm_norm_g[l])
        y_b = nsa_branch(n_q, n_g, n_kv, cmp_k_pos[l], cmp_k_w1[l], cmp_k_w2[l], cmp_v_pos[l], cmp_v_w1[l], cmp_v_w2[l])
        merged = jax.nn.sigmoid(gate_a) * (y_a @ w_branch_a[l]) + jax.nn.sigmoid(gate_b) * (y_b @ w_branch_b[l])
        x = x + merged @ w_out[l]
        h = rms_norm(x, norm2_g[l])
        x = x + (jax.nn.silu(h @ w_ffn_gate[l]) * (h @ w_ffn_up[l])) @ w_ffn_down[l]
    return rms_norm(x, norm_f_g)
```

```python
import numpy as np
import ml_dtypes
import concourse.bass as bass
import concourse.mybir as mybir
from concourse.bass_utils import run_bass_kernel_spmd
from contextlib import ExitStack

F32 = mybir.dt.float32
BF16 = mybir.dt.bfloat16
AF = mybir.ActivationFunctionType
ALU = mybir.AluOpType

ENGS = ("pe", "act", "dve", "pool", "sp")
SAME_SYNC = True
SEM_EPOCH = 16000

TPG = 2
GT = 128 * TPG
BIG = 30000.0
D = 1024
FF = 2816
NHC = 22
IN_W = 6432

C_MQ, C_MK, C_NQ, C_GA, C_GB, C_KSLC, C_KWIN, C_KCMP, C_VCMP = 0, 512, 1024, 1536, 2560, 3584, 3712, 3840, 3968
C_V, C_O, C_VSLC, C_VWIN, C_NG, C_I, C_F = 4096, 5120, 6144, 6272, 6400, 6424, 6428
R_BCM, R_G1, R_G2, R_GF, R_CW, R_CB, R_BI, R_BF, R_FB = 0, 32, 40, 48, 56, 88, 96, 97, 98
F_ID, F_PMUL, F_PADD, F_PELIG, F_TRI, F_ONES, NF = 0, 128, 384, 640, 896, 1024, 1536
B_ID, B_TN, B_TN2, B_MC, B_W, B_ONES, NB = 0, 128, 256, 384, 384 + 17 * 128, 384 + 17 * 128 + 4096, 384 + 17 * 128 + 4096 + 128


class Buf:
    __slots__ = ("name", "w", "rs", "semkey", "semval")

    def __init__(self, name):
        self.name = name
        self.w = None
        self.rs = {}
        self.semkey = None
        self.semval = 0


class TB:
    __slots__ = ("t", "bs")

    def __init__(self, t, bs):
        self.t = t
        self.bs = bs


def _flat(tbs):
    out = []
    for x in tbs:
        if isinstance(x, TB):
            out.extend(x.bs)
        elif isinstance(x, Buf):
            out.append(x)
        else:
            out.extend(_flat(x))
    return out


class Prog:
    def __init__(self, nc, stack):
        self.nc = nc
        self.stack = stack
        self.ops = {e: [] for e in ENGS}
        self.cnt = {e: 0 for e in ENGS}
        self.waited = {e: {} for e in ENGS}
        self.sems = {}
        for e in ENGS:
            self.sems[e] = stack.enter_context(nc.semaphore("s_" + e))
        self.nbuf = 0
        self.cap = None

    def buf(self, name="b"):
        self.nbuf += 1
        return Buf("%s%d" % (name, self.nbuf))

    def sb(self, name, shape, dtype):
        t = self.stack.enter_context(self.nc.sbuf_tensor("sb_" + name, list(shape), dtype))
        return TB(t, [self.buf(name)])

    def ps(self, name, shape, dtype=F32):
        t = self.stack.enter_context(self.nc.psum_tensor("ps_" + name, list(shape), dtype))
        return TB(t, [self.buf(name)])

    def _record(self, eng, fn, reads, writes, token, inc):
        deps = {}

        def add(tok):
            if tok is None:
                return
            k, v = tok
            if deps.get(k, 0) < v:
                deps[k] = v
        for b in reads:
            add(b.w)
        for b in writes:
            add(b.w)
            for k, v in b.rs.items():
                add((k, v))
        waits = []
        wd = self.waited[eng]
        for k, v in deps.items():
            if k.split("#")[0] == eng and (eng == "pe" or not SAME_SYNC):
                continue
            if wd.get(k, 0) >= v:
                continue
            wd[k] = v
            waits.append((k, v))
        self.ops[eng].append((waits, fn, token, inc))
        if token is not None:
            for b in reads:
                if b.rs.get(token[0], 0) < token[1]:
                    b.rs[token[0]] = token[1]
            for b in writes:
                b.w = token
                b.rs = {}

    def capture(self, f):
        self.cap = []
        f()
        lst = self.cap
        self.cap = None
        return lst

    def replay(self, lst):
        for kind, eng, fn, reads, writes, sembuf, cost in lst:
            if kind == "op":
                self.op(eng, fn, reads, writes)
            else:
                self.dma(eng, fn, reads, writes, sembuf)

    @staticmethod
    def merge(a, b):
        LAT = 1.0
        eng_free = {}
        bw = {}
        br = {}
        out = []

        def ready(o):
            kind, eng, fn, reads, writes, sembuf, cost = o
            t = eng_free.get(eng, 0.0)
            for bf in _flat(reads):
                w = bw.get(id(bf))
                if w is not None:
                    t = max(t, w[0] + (LAT if w[1] != eng else 0.05))
            for bf in _flat(writes):
                w = bw.get(id(bf))
                if w is not None:
                    t = max(t, w[0] + (LAT if w[1] != eng else 0.05))
                r = br.get(id(bf))
                if r is not None:
                    t = max(t, r + LAT)
            return t

        def commit(o, t):
            kind, eng, fn, reads, writes, sembuf, cost = o
            if kind == "dma":
                eng_free[eng] = t + 0.1
                fin = t + 3.0
            else:
                fin = t + cost
                eng_free[eng] = fin
            for bf in _flat(reads):
                br[id(bf)] = max(br.get(id(bf), 0.0), fin)
            for bf in _flat(writes):
                bw[id(bf)] = (fin, eng)
                br.pop(id(bf), None)
            out.append(o)

        i = j = 0
        while i < len(a) or j < len(b):
            ta = ready(a[i]) if i < len(a) else float("inf")
            tb = ready(b[j]) if j < len(b) else float("inf")
            if ta <= tb:
                commit(a[i], ta)
                i += 1
            else:
                commit(b[j], tb)
                j += 1
        return out

    def op(self, eng, fn, reads=(), writes=(), cost=0.3):
        if self.cap is not None:
            self.cap.append(("op", eng, fn, reads, writes, None, cost))
            return
        c = self.cnt[eng]
        self.cnt[eng] += 1
        ep = c // SEM_EPOCH
        key = eng if ep == 0 else "%s#%d" % (eng, ep)
        if key not in self.sems:
            self.sems[key] = self.stack.enter_context(self.nc.semaphore("s_%s_%d" % (eng, ep)))
        token = (key, c % SEM_EPOCH + 1)
        self._record(eng, fn, _flat(reads), _flat(writes), token, 1)

    def dma(self, eng, fn, reads=(), writes=(), sembuf=None):
        if self.cap is not None:
            self.cap.append(("dma", eng, fn, reads, writes, sembuf, 3.0))
            return
        if sembuf.semkey is None:
            sembuf.semkey = "d_" + sembuf.name
            self.sems[sembuf.semkey] = self.stack.enter_context(self.nc.semaphore(sembuf.semkey))
        sembuf.semval += 16
        token = (sembuf.semkey, sembuf.semval)
        self._record(eng, fn, _flat(reads), _flat(writes), token, 16)

    def fence(self, eng, tbs):
        bs = _flat(tbs)
        self._record(eng, None, bs, bs, None, 0)

    def emit(self):
        nc = self.nc
        sems = self.sems
        ops = self.ops

        def run(name, e):
            for waits, fn, token, inc in ops[name]:
                for k, v in waits:
                    e.wait_ge(sems[k], v)
                if fn is not None:
                    ins = fn(e)
                    ins.then_inc(sems[token[0]], inc)

        with nc.Block() as block:
            @block.tensor
            def _(e):
                run("pe", e)

            @block.scalar
            def _(e):
                run("act", e)

            @block.vector
            def _(e):
                run("dve", e)

            @block.gpsimd
            def _(e):
                run("pool", e)

            @block.sync
            def _(e):
                run("sp", e)


class Arena:
    GR = 512

    def __init__(self, P, name, nbytes):
        self.P = P
        self.nbytes = nbytes
        self.t = P.stack.enter_context(P.nc.sbuf_tensor("sb_" + name, [128, nbytes // 2], BF16))
        self.g = [P.buf(name + "g") for _ in range((nbytes + self.GR - 1) // self.GR)]

    def view(self, off, shape, dtype):
        esz = 4 if dtype == F32 else 2
        n = int(np.prod(shape[1:]))
        nb = n * esz
        assert off % 4 == 0 and off + nb <= self.nbytes, (off, nb, self.nbytes)
        ap = self.t[:, off // 2:(off + nb) // 2]
        if dtype == F32:
            ap = ap.bitcast(F32)
        if len(shape) == 3:
            ap = ap.rearrange("p (a b) -> p a b", a=shape[1])
        elif len(shape) == 4:
            ap = ap.rearrange("p (a b c) -> p a b c", a=shape[1], b=shape[2])
        return TB(ap, self.g[off // self.GR:(off + nb - 1) // self.GR + 1])


class Bump:
    def __init__(self, arena, base):
        self.a = arena
        self.off = base

    def take(self, shape, dtype):
        esz = 4 if dtype == F32 else 2
        nb = int(np.prod(shape[1:])) * esz
        v = self.a.view(self.off, shape, dtype)
        self.off += (nb + Arena.GR - 1) // Arena.GR * Arena.GR
        return v


class Pool:
    def __init__(self, items):
        self.items = items
        self.i = 0

    def next(self):
        x = self.items[self.i % len(self.items)]
        self.i += 1
        return x


def make_perm():
    O_V, O_O, O_I, O_F, O_NQ, O_NKV, O_NG, O_GA, O_GB = 1024, 2048, 3072, 3076, 3080, 3592, 4360, 4384, 5408
    r = np.arange
    p = [r(0, 1024)]
    for i in range(4):
        p += [O_NQ + i * 64 + r(64), O_NQ + (4 + i) * 64 + r(64)]
    p += [O_GA + r(1024), O_GB + r(1024)]
    p += [O_NKV + 256 + r(128), O_NKV + 512 + r(128), O_NKV + r(128), O_NKV + 128 + r(128)]
    p += [O_V + r(1024), O_O + r(1024), O_NKV + 384 + r(128), O_NKV + 640 + r(128), O_NG + r(24), O_I + r(4), O_F + r(4)]
    p = np.concatenate(p)
    assert p.shape[0] == IN_W and len(set(p.tolist())) == IN_W
    return p


def make_consts(S):
    q = np.arange(128)
    cf = np.zeros((128, NF), np.float32)
    cf[:, F_ID:F_ID + 128] = np.eye(128)
    xx = np.arange(256)
    rel = (xx[None, :] - 128) - (q[:, None] // 64)
    cf[:, F_PMUL:F_PMUL + 256] = (rel <= -2)
    cf[:, F_PADD:F_PADD + 256] = np.where((rel == 0) | (rel == -1), 1e9, np.where(rel > 0, -1e9, 0.0))
    cf[:, F_PELIG:F_PELIG + 256] = (rel <= 0)
    cf[:, F_TRI:F_TRI + 128] = (q[:, None] <= q[None, :])
    cf[:, F_ONES:NF] = 1.0
    cb = np.zeros((128, NB), np.float32)
    cb[:, B_ID:B_ID + 128] = np.eye(128)
    cb[:, B_TN:B_TN + 128] = np.where(q[:, None] > q[None, :], -BIG, 0.0)
    cb[:, B_TN2:B_TN2 + 128] = np.where(q[:, None] <= q[None, :], -BIG, 0.0)
    for m in range(17):
        cb[:, B_MC + m * 128:B_MC + (m + 1) * 128] = np.where(16 * q[:, None] + 31 - 128 * m > q[None, :], -BIG, 0.0)
    y = np.arange(4096)
    cb[:, B_W:B_W + 4096] = ((q[:, None] % 64) == (y[None, :] // 64))
    cb[:, B_ONES:NB] = 1.0
    cpad = S // 16
    ncmp = cpad - 1
    c = np.arange(cpad)
    j = np.arange(128)
    dlt = c[:, None] - 4 * j[None, :]
    mm = np.where((dlt == -1) | (dlt == 3), 1.0, np.where((dlt >= 0) & (dlt <= 2), 2.0, 0.0))
    mm = mm * (c[:, None] < ncmp)
    nct = cpad // 128
    mm = mm.reshape(nct, 128, 128).transpose(1, 0, 2)
    return cf, cb.astype(ml_dtypes.bfloat16), np.ascontiguousarray(mm).astype(ml_dtypes.bfloat16)


def build(S, dbg=False):
    NT = S // 128
    NG = NT // TPG
    CPAD = S // 16
    NCMP = CPAD - 1
    NCT = CPAD // 128
    nc = bass.Bass("TRN2", target_bir_lowering=False)

    def din(name, shape, dt=F32):
        return nc.dram_tensor(name, list(shape), dt, kind="ExternalInput").ap()

    x_d = din("x", [S, D])
    win_d = din("w_in", [D, IN_W])
    pv_d = din("pv", [128, 128])
    brow_d = din("brow", [1, 2328])
    gm_d = din("gm", [1, 1024])
    ckw1_d = din("ckw1", [2048, 256])
    ckw2_d = din("ckw2", [256, 64])
    cvw1_d = din("cvw1", [2048, 256])
    cvw2_d = din("cvw2", [256, 64])
    ckpos_d = din("ckposT", [64, 32])
    cvpos_d = din("cvposT", [64, 32])
    wa_d = din("wa", [1024, 1024])
    wb_d = din("wb", [512, 1024])
    wo_d = din("wo", [1024, 1024])
    wg_d = din("wg", [1024, FF])
    wu_d = din("wu", [1024, FF])
    wd_d = din("wd", [FF, 1024])
    cf_d = din("constf", [128, NF])
    cb_d = din("constb", [128, NB], BF16)
    mmap_d = din("mmap", [128, NCT, 128], BF16)
    y_d = nc.dram_tensor("y", [S // 2, D], F32, kind="ExternalOutput").ap()
    pc_d = din("pc", [128, 260])
    if dbg:
        dya_d = nc.dram_tensor("dbg_ya", [S // 2, 1024], BF16, kind="ExternalOutput").ap()
        dyb_d = nc.dram_tensor("dbg_yb", [S // 2, 512], F32, kind="ExternalOutput").ap()

    def dscr(name, shape):
        return nc.dram_tensor(name, list(shape), BF16).ap()

    ckw1b_d = dscr("ckw1b", [2048, 256])
    ckw2b_d = dscr("ckw2b", [256, 64])
    cvw1b_d = dscr("cvw1b", [2048, 256])
    cvw2b_d = dscr("cvw2b", [256, 64])

    st = ExitStack()
    with st:
        P = Prog(nc, st)

        def MM(out, lhsT, rhs, start=True, stop=True, r=(), w=(), skip=False):
            n = int(np.prod(rhs.shape[1:]))
            P.op("pe", lambda e: e.matmul(out, lhsT=lhsT, rhs=rhs, start=start, stop=stop, skip_group_check=skip), r, w,
                 cost=(0.06 if n <= 65 else 0.12 + 0.00028 * n))

        def TR(out, in_, ident, r=(), w=()):
            P.op("pe", lambda e: e.transpose(out, in_, ident), r, w, cost=0.19)

        def ACT(out, in_, func, r=(), w=(), scale=1.0, bias=None, accum=None):
            def f(e):
                kw = {}
                if bias is not None:
                    kw["bias"] = bias
                if accum is not None:
                    kw["accum_out"] = accum
                return e.activation(out=out, in_=in_, func=func, scale=scale, **kw)
            P.op("act", f, r, w, cost=0.25 + 0.0006 * int(np.prod(out.shape[1:])))

        def TS(eng, out, in0, s1, s2, op0, op1=None, r=(), w=()):
            def f(e):
                if op1 is None:
                    return e.tensor_scalar(out=out, in0=in0, scalar1=s1, scalar2=None, op0=op0)
                return e.tensor_scalar(out=out, in0=in0, scalar1=s1, scalar2=s2, op0=op0, op1=op1)
            P.op(eng, f, r, w, cost=0.2 + 0.0007 * int(np.prod(out.shape[1:])))

        def TT(eng, out, in0, in1, op, r=(), w=()):
            P.op(eng, lambda e: e.tensor_tensor(out=out, in0=in0, in1=in1, op=op), r, w, cost=0.2 + 0.0009 * int(np.prod(out.shape[1:])))

        def STT(eng, out, in0, scalar, in1, op0, op1, r=(), w=()):
            P.op(eng, lambda e: e.scalar_tensor_tensor(out=out, in0=in0, scalar=scalar, in1=in1, op0=op0, op1=op1), r, w,
                 cost=0.2 + 0.0009 * int(np.prod(out.shape[1:])))

        def CP(eng, out, in_, r=(), w=()):
            if eng == "act":
                ACT(out, in_, AF.Copy, r, w)
            else:
                P.op(eng, lambda e: e.tensor_copy(out=out, in_=in_), r, w, cost=0.15 + 0.0005 * int(np.prod(out.shape[1:])))

        def MS(eng, ap, val, w=()):
            P.op(eng, lambda e: e.memset(ap, val), (), w)

        def DMA(eng, out, in_, r, w, sem):
            P.dma(eng, lambda e: e.dma_start(out=out, in_=in_), r, w, sem)

        def dram_tb(name):
            return TB(None, [P.buf(name)])

        cf = P.sb("cf", [128, NF], F32)
        cb = P.sb("cb", [128, NB], BF16)
        pvs = P.sb("pvs", [128, 128], F32)
        pvt = P.sb("pvt", [128, 128], F32)
        brow = P.sb("brow", [1, 2328], BF16)
        gmbc = P.sb("gmbc", [128, 1024], F32)
        DMA("sp", cf.t[:], cf_d, [], [cf], cf.bs[0])
        DMA("sp", cb.t[:], cb_d, [], [cb], cb.bs[0])
        DMA("sp", pvs.t[:], pv_d, [], [pvs], pvs.bs[0])
        DMA("pool", brow.t[:], brow_d, [], [brow], brow.bs[0])
        pcs = P.sb("pcs", [128, 260], F32)
        DMA("sp", pcs.t[:], pc_d, [], [pcs], pcs.bs[0])
        identf = cf.t[:, F_ID:F_ID + 128]
        identb = cb.t[:, B_ID:B_ID + 128]
        onesf = cf.t[:, F_ONES:F_ONES + 128]
        tri = cf.t[:, F_TRI:F_TRI + 128]

        banks = [P.ps("bank%d" % i, [128, 512], F32) for i in range(8)]
        mmp = Pool(banks[0:2])
        atp = Pool([banks[2], banks[3], banks[6], banks[7]])
        ocA, ocB = banks[4], banks[5]
        mlp = Pool([banks[0], banks[1]])
        bigp = Pool(banks[0:6])

        def bfv(bank):
            return bank.t[:, 0:512].bitcast(BF16)

        bk = mmp.next()
        TR(bk.t[:, 0:128], pvs.t[:], identf, [pvs, cf], [bk])
        CP("dve", pvt.t[:], bk.t[:, 0:128], [bk], [pvt])
        def pcol(c):
            return pvt.t[:, c:c + 1]

        negbf = P.sb("negbf", [4, 1], F32)
        TT("dve", negbf.t[:], pvt.t[0:4, R_BF:R_BF + 1], pvt.t[0:4, R_FB:R_FB + 1], ALU.add, [pvt], [negbf])
        TS("dve", negbf.t[:], negbf.t[:], -1.0, None, ALU.mult, None, [negbf], [negbf])

        kslcT = P.sb("kslcT", [128, S], BF16)
        kslc_b = [P.buf("kslc") for _ in range(NT)]
        vslc = P.sb("vslc", [128, NT, 2, 65], BF16)
        vslc_b = [P.buf("vslc") for _ in range(NT)]
        kwinT = P.sb("kwinT", [128, 8 * 128], BF16)
        kwin_b = [P.buf("kwin") for _ in range(8)]
        vwin = P.sb("vwin", [128, 8, 2, 65], BF16)
        vwin_b = [P.buf("vwin") for _ in range(8)]
        kcT = P.sb("kcT", [128, CPAD], BF16)
        vcx = P.sb("vcx", [128, NCT, 2, 193], BF16)
        stateF = P.sb("stateF", [128, 4, 257], F32)
        state_b = [P.buf("st") for _ in range(4)]
        Cbf = P.sb("Cbf", [128, 4, 258], BF16)
        cbf_b = [P.buf("cbf") for _ in range(4)]
        halo = P.sb("halo", [128, 8, 3], F32)
        MS("pool", vslc.t[:], 1.0, [vslc_b])
        MS("pool", vwin.t[:], 1.0, [vwin_b])
        MS("pool", kcT.t[:], 0.0, [kcT])
        MS("pool", vcx.t[:], 1.0, [vcx])
        MS("pool", stateF.t[:], 0.0, [state_b])
        MS("pool", Cbf.t[:], 0.0, [cbf_b])
        MS("pool", halo.t[:], 0.0, [halo])
        for gi in range(2):
            DMA("sp", vcx.t[:, :, gi, 65:193], mmap_d, [], [vcx], vcx.bs[0])

        def small(name, shape, dt=F32):
            return P.sb(name, shape, dt)

        scT = small("scT", [128, TPG, 12])
        ssx = small("ssx", [128, 4])
        decbc = small("decbc", [128, TPG * 4])
        dec4 = small("dec4", [4, TPG])
        rdg = small("rdg", [4, TPG * 4])
        t1s = [small("t1s%d" % i, [128, 4]) for i in range(2)]
        ssq = [small("ssq%d" % i, [128, 1]) for i in range(2)]
        rs4 = small("rs4", [128, 4])
        w4 = small("w4", [128, 4])
        m8 = small("m8", [128, 16])
        gbrs = [small("gbr%d" % i, [128, TPG, 24]) for i in range(2)]
        negselT = [small("negselT%d" % i, [128, 256], BF16) for i in range(2)]
        selbb = small("selbb", [128, 128], BF16)
        ktok = [small("ktok%d" % i, [128, 128], BF16) for i in range(2)]
        stm = [small("stm%d" % i, [128, 128], BF16) for i in range(2)]
        junk = small("junk", [128, 256], BF16)
        cconst = small("cconst", [128, 2])
        posT = small("posT", [128, 32], F32)
        posTb = small("posTb", [128, 32], BF16)
        w2s = small("w2s", [128, 2, 64], BF16)

        OB = 4
        NTL = 128 * OB
        AR_BYTES = 126 * 1024
        ar = Arena(P, "arena", AR_BYTES)
        NSLOT = 5
        SLOTB = 8192
        ring = [ar.view(i * SLOTB, [128, SLOTB // 2], BF16) for i in range(NSLOT)]
        ringi = [0]
        al = Bump(ar, NSLOT * SLOTB)
        xl = [al.take([128, 1024], F32) for _ in range(TPG)]
        hnT = al.take([128, 8, GT], BF16)
        assert len(hnT.bs) == 8
        hnc = [TB(None, [hnT.bs[i]]) for i in range(8)]
        sqj = al.take([128, 1024], BF16)
        xTt = [al.take([128, NTL], F32) for _ in range(8)]
        yaT_t = al.take([128, 8, NTL], BF16)
        ybT_t = al.take([128, 4, NTL], BF16)
        base_phase = al.off
        a1 = Bump(ar, 0)
        kcmpT = a1.take([128, 16, S // 16], BF16)
        vcmpT = a1.take([128, 16, S // 16], BF16)
        a1 = Bump(ar, max(base_phase, a1.off))
        wcmp = a1.take([128, 8, 512], BF16)
        w1s = a1.take([128, 32, 256], BF16)
        hT = a1.take([128, 2, CPAD], BF16)
        gmrow = ar.view(base_phase, [128, 1024], F32)
        a2 = Bump(ar, base_phase)
        vt = a2.take([128, TPG, 4, 258], BF16)
        sigo = a2.take([128, 1024], F32)
        nqT = a2.take([128, 4, 128], BF16)
        qkT = a2.take([128, 8, GT], BF16)
        pre = [a2.take([128, GT + 3], F32) for _ in range(2)]
        cvt = [a2.take([128, GT], F32) for _ in range(2)]
        ya = a2.take([128, 1024], BF16)
        yb = a2.take([128, 512], F32)
        gA = a2.take([128, GT], F32)
        gL = a2.take([128, GT], F32)
        gB = a2.take([128, GT], F32)
        gG = a2.take([128, GT], F32)
        gN = a2.take([128, GT], F32)
        gek = a2.take([128, GT], F32)
        geq = a2.take([128, GT], F32)
        gem = a2.take([128, GT], F32)
        dtmp = [a2.take([128, 257], F32) for _ in range(2)]
        hb = [a2.take([128, 256], F32) for _ in range(2)]
        ptb = [a2.take([128, 512], BF16) for _ in range(4)]
        impb = a2.take([128, 128], F32)
        scb = a2.take([128, 128], F32)
        sc2b = a2.take([128, 128], F32)
        selb = a2.take([128, 128], F32)
        ybtmp = a2.take([128, 256], F32)
        mix_end = a2.off
        a3 = Bump(ar, base_phase)
        hnTt = a3.take([128, 8, NTL], BF16)
        mergedT = a3.take([128, 8, NTL], BF16)
        actb = [a3.take([128, 4, NTL], BF16) for _ in range(2)]
        tmpm = [a3.take([128, NTL], F32) for _ in range(2)]
        rsbT = a3.take([128, NTL], F32)
        sqtT = [a3.take([128, NTL], F32) for _ in range(2)]
        ot = a3.take([128, 1024], F32)
        print("arena use: phase1 %d  mixer %d  tail %d  of %d" % (a1.off, mix_end, a3.off, AR_BYTES))
        assert max(a1.off, a2.off, a3.off) <= AR_BYTES
        ptp = Pool(ptb)

        DMA("sp", gmrow.t[0:1, :], gm_d, [], [gmrow], gmrow.bs[0])
        for hf in range(2):
            bk = mmp.next()
            MM(bk.t[:, 0:512], cf.t[0:1, F_ONES:F_ONES + 128], gmrow.t[0:1, hf * 512:(hf + 1) * 512], True, True, [cf, gmrow], [bk])
            CP("dve", gmbc.t[:, hf * 512:(hf + 1) * 512], bk.t[:, 0:512], [bk], [gmbc])

        TS("dve", gmbc.t[:], gmbc.t[:], 0.5, None, ALU.mult, None, [gmbc], [gmbc])
        Bcar = small("Bcar", [4, 1])
        Gcar = small("Gcar", [4, 1])
        NGcar = small("NGcar", [4, 1])
        MS("pool", Bcar.t[:], 0.0, [Bcar])
        MS("pool", Gcar.t[:], 0.0, [Gcar])
        MS("pool", NGcar.t[:], 0.0, [NGcar])

        def wtile(blk):
            src_ap, a, b, src_tb = blk
            slot = ring[ringi[0] % NSLOT]
            ringi[0] += 1
            assert a * b * 2 <= SLOTB
            v2 = slot.t[:, 0:a * b]
            DMA("sp", v2, src_ap, [src_tb], [slot], slot.bs[0])
            return TB(v2.rearrange("p (a b) -> p a b", a=a), slot.bs)

        def load_x(g):
            for ti in range(TPG):
                r0 = (g * TPG + ti) * 128
                DMA("sp", xl[ti].t, x_d[r0:r0 + 128, :], [], [xl[ti]], xl[ti].bs[0])

        def prep_x(g, nxt, own_cj, p1=False):
            if own_cj is not None:
                for hf in range(2):
                    bk = mmp.next()
                    for c in range(4):
                        TR(bk.t[:, c * 128:(c + 1) * 128], xl[1].t[:, (hf * 4 + c) * 128:(hf * 4 + c + 1) * 128], identf, [xl[1], cf], [bk])
                    for c in range(4):
                        dc = hf * 4 + c
                        CP("act" if hf else "dve", xTt[dc].t[:, own_cj:own_cj + 128], bk.t[:, c * 128:(c + 1) * 128], [bk], [xTt[dc]])
            for ti in range(TPG):
                ACT(sqj.t, xl[ti].t, AF.Square, [xl[ti]], [sqj, ssx], accum=ssx.t[:, ti:ti + 1])
            ACT(ssx.t[:, 2:4], ssx.t[:, 0:2], AF.Ln, [ssx], [ssx], scale=1.0 / 1024, bias=1e-6)
            ACT(ssx.t[:, 2:4], ssx.t[:, 2:4], AF.Exp, [ssx], [ssx], scale=-0.5)
            for ti in range(TPG):
                if p1:
                    TS("dve", xl[ti].t, xl[ti].t, ssx.t[:, 2 + ti:3 + ti], None, ALU.mult, None, [xl[ti], ssx], [xl[ti]])
                else:
                    ACT(xl[ti].t, xl[ti].t, AF.Copy, [xl[ti], ssx], [xl[ti]], scale=ssx.t[:, 2 + ti:3 + ti])
            for dc in range(8):
                bk = mmp.next()
                for ti in range(TPG):
                    TR(bk.t[:, ti * 128:(ti + 1) * 128], xl[ti].t[:, dc * 128:(dc + 1) * 128], identf, [xl[ti], cf], [bk])
                if p1 and dc % 2 == 0:
                    TS("dve", hnT.t[:, dc, :], bk.t[:, 0:GT], pcol(R_G1 + dc), None, ALU.mult, None, [bk, pvt], [hnc[dc]])
                else:
                    ACT(hnT.t[:, dc, :], bk.t[:, 0:GT], AF.Copy, [bk, pvt], [hnc[dc]], scale=pcol(R_G1 + dc))
            if nxt is not None:
                load_x(nxt)

        def cast(dst, src, rows, step, name):
            tb = dram_tb(name)
            for r0 in range(0, rows, step):
                r1 = min(rows, r0 + step)
                DMA("pool", dst[r0:r1, :], src[r0:r1, :], [], [tb], tb.bs[0])
            return tb

        def cast_tiled(src_d, kcn, widths, name, first=()):
            nmax = max(widths)
            scr = nc.dram_tensor(name, [len(widths), 128, kcn * nmax], BF16).ap()
            offs = [sum(widths[:b]) for b in range(len(widths))]
            blocks = [None] * len(widths)
            order = list(first) + [b for b in range(len(widths)) if b not in first]
            for b in order:
                n = widths[b]
                c = offs[b]
                tb = dram_tb("%s_%d" % (name, b))
                dst = scr[b, :, 0:kcn * n].rearrange("p (kc n) -> p kc n", kc=kcn)
                srcv = src_d[:, c:c + n].rearrange("(kc p) n -> p kc n", p=128)
                DMA("pool", dst, srcv, [], [tb], tb.bs[0])
                blocks[b] = (scr[b, :, 0:kcn * n], kcn, n, tb)
            return blocks

        winT = cast_tiled(win_d, 8, [512] * 12 + [288], "winT", first=(7,))
        waT = cast_tiled(wa_d, 8, [512, 512], "waT")
        wbT = cast_tiled(wb_d, 4, [1024], "wbT")
        woT = cast_tiled(wo_d, 8, [512, 512], "woT")
        wgT = cast_tiled(wg_d, 8, [512] * 5 + [256], "wgT")
        wuT = cast_tiled(wu_d, 8, [512] * 5 + [256], "wuT")
        def cast_rows(src_d, blocks, ncols, name):
            scr = nc.dram_tensor(name, [len(blocks), 128, max(blocks) * ncols], BF16).ap()
            tb = dram_tb(name)
            r = 0
            out = []
            for b, nh in enumerate(blocks):
                dst = scr[b, :, 0:nh * ncols].rearrange("p (hc n) -> p hc n", hc=nh)
                srcv = src_d[r:r + nh * 128, :].rearrange("(hc p) n -> p hc n", p=128)
                DMA("pool", dst, srcv, [], [tb], tb.bs[0])
                out.append((scr[b, :, 0:nh * ncols], nh, ncols, tb))
                r += nh * 128
            return out

        wd2T = cast_rows(wd_d, [4, 4, 4, 4, 4, 2], 1024, "wd2T")
        def cast_w1(dst, src, name):
            tb = dram_tb(name)
            DMA("pool", dst.rearrange("(d l) h -> d l h", l=32), src.rearrange("(l d) h -> d l h", d=64), [], [tb], tb.bs[0])
            return tb

        ckw1b = cast_w1(ckw1b_d, ckw1_d, "ckw1b")
        ckw2b = cast(ckw2b_d, ckw2_d, 256, 256, "ckw2b")
        cvw1b = cast_w1(cvw1b_d, cvw1_d, "cvw1b")
        cvw2b = cast(cvw2b_d, cvw2_d, 256, 256, "cvw2b")

        mmp.items = list(banks[0:6])
        DMA("sp", wcmp.t, winT[7][0].rearrange("p (kc n) -> p kc n", kc=8), [winT[7][3]], [wcmp], wcmp.bs[0])
        load_x(0)
        for g in range(NG):
            prep_x(g, (g + 1) % NG, None, p1=True)
            for oc in range(2):
                bk = bigp.next()
                for kc in range(8):
                    MM(bk.t[:, 0:GT], wcmp.t[:, kc, 256 + oc * 128:256 + (oc + 1) * 128], hnT.t[:, kc, :], kc == 0, kc == 7, [wcmp, hnc[kc]], [bk])
                dst = kcmpT if oc == 0 else vcmpT
                q0 = g * (GT // 16)
                dsti = dst.t[:, :, q0:q0 + GT // 16]
                srci = bk.t[:, 0:GT].rearrange("p (q r) -> p r q", r=16)
                if oc == 0:
                    ACT(dsti, srci, AF.Identity, [bk, pvt], [dst], bias=pcol(R_BCM + 30 + oc))
                else:
                    TS("dve", dsti, srci, pcol(R_BCM + 30 + oc), None, ALU.add, None, [bk, pvt], [dst])

        MS("pool", hT.t, 0.0, [hT])
        for kv in range(2):
            w1b_d, w1tb = (ckw1b_d, ckw1b) if kv == 0 else (cvw1b_d, cvw1b)
            w2b_d, w2tb = (ckw2b_d, ckw2b) if kv == 0 else (cvw2b_d, cvw2b)
            pos_d = ckpos_d if kv == 0 else cvpos_d
            src = kcmpT if kv == 0 else vcmpT
            for half in range(2):
                DMA("sp", w1s.t[half * 64:(half + 1) * 64, :, :], w1b_d.rearrange("(d l) h -> d l h", l=32), [w1tb], [w1s], w1s.bs[0])
                DMA("sp", posT.t[half * 64:(half + 1) * 64, :], pos_d, [], [posT], posT.bs[0])
            DMA("sp", w2s.t[:], w2b_d.rearrange("(hc p) d -> p hc d", p=128), [w2tb], [w2s], w2s.bs[0])
            CP("dve", posTb.t[:], posT.t[:], [posT], [posTb])
            for hc in range(2):
                bk = mmp.next()
                for l in range(32):
                    MM(bk.t[:, 0:1], w1s.t[0:64, l, hc * 128:(hc + 1) * 128], posTb.t[0:64, l:l + 1], l == 0, l == 31, [w1s, posTb], [bk])
                CP("dve", cconst.t[:, hc:hc + 1], bk.t[:, 0:1], [bk], [cconst])
            for gi in range(2):
                pb = 64 * gi
                for hc in range(2):
                    for c0 in range(0, NCMP, 512):
                        n = min(512, NCMP - c0)
                        bk = mmp.next()
                        for l in range(32):
                            a_, r_ = divmod(l, 16)
                            MM(bk.t[:, 0:n], w1s.t[pb:pb + 64, l, hc * 128:(hc + 1) * 128],
                               src.t[pb:pb + 64, r_, c0 + a_:c0 + a_ + n], l == 0, l == 31, [w1s, src], [bk])
                        ACT(hT.t[:, hc, c0:c0 + n], bk.t[:, 0:n], AF.Silu, [bk, cconst], [hT], bias=cconst.t[:, hc:hc + 1])
                if kv == 0:
                    for c0 in range(0, NCMP, 512):
                        n = min(512, NCMP - c0)
                        bk = mmp.next()
                        for hc in range(2):
                            MM(bk.t[pb:pb + 64, 0:n], w2s.t[:, hc, :], hT.t[:, hc, c0:c0 + n], hc == 0, hc == 1, [w2s, hT], [bk])
                        CP("dve", kcT.t[pb:pb + 64, c0:c0 + n], bk.t[pb:pb + 64, 0:n], [bk], [kcT])
                else:
                    for ct in range(NCT):
                        bk = mmp.next()
                        for hc in range(2):
                            MM(bk.t[:, 0:64], hT.t[:, hc, ct * 128:(ct + 1) * 128], w2s.t[:, hc, :], hc == 0, hc == 1, [w2s, hT], [bk])
                        CP("dve", vcx.t[:, ct, gi, 0:64], bk.t[:, 0:64], [bk], [vcx])

        def bias_mm(bk, n, boff):
            MM(bk.t[:, 0:n], cb.t[0:1, B_ONES:B_ONES + 128], brow.t[0:1, boff:boff + n], False, True, [cb, brow], [bk])

        def ytr_a(g):
            cj = (g % OB) * 128
            bk = mmp.next()
            for c in range(8):
                TR(bfv(bk)[:, c * 128:(c + 1) * 128], ya.t[:, c * 128:(c + 1) * 128], identb, [ya, cb], [bk])
            CP("dve", yaT_t.t[:, :, cj:cj + 128], bfv(bk)[:, 0:1024].rearrange("p (a b) -> p a b", a=8), [bk], [yaT_t])

        def ytr_b(g):
            cj = (g % OB) * 128
            bk = mmp.next()
            for c in range(4):
                TR(bk.t[:, c * 128:(c + 1) * 128], yb.t[:, c * 128:(c + 1) * 128], identf, [yb, cf], [bk])
            CP("dve", ybT_t.t[:, :, cj:cj + 128], bk.t[:, 0:512].rearrange("p (a b) -> p a b", a=4), [bk], [ybT_t])

        def dense_tail(g0):
            N = NTL

            def normT(gcol, out_fn):
                bk = bigp.next()
                for dc in range(8):
                    sq = sqtT[dc % 2]
                    ACT(sq.t, xTt[dc].t, AF.Square, [xTt[dc]], [sq])
                    MM(bk.t[:, 0:N], onesf, sq.t, dc == 0, dc == 7, [sq, cf], [bk])
                ACT(rsbT.t, bk.t[:, 0:N], AF.Ln, [bk], [rsbT], scale=1.0 / 1024, bias=1e-6)
                ACT(rsbT.t, rsbT.t, AF.Exp, [rsbT], [rsbT], scale=-0.5)
                for dc in range(8):
                    o = out_fn(dc)
                    STT("dve", o[0], xTt[dc].t, pcol(gcol + dc), rsbT.t, ALU.mult, ALU.mult, [xTt[dc], rsbT, pvt], [o[1]])

            def proj(wblk, oc, rhsT, nk):
                bk = bigp.next()
                for kc in range(nk):
                    MM(bk.t[:, 0:N], wblk.t[:, kc, oc * 128:(oc + 1) * 128], rhsT.t[:, kc, :], kc == 0, kc == nk - 1, [wblk, rhsT], [bk])
                return bk

            normT(R_G1, lambda dc: (hnTt.t[:, dc, :], hnTt))
            for blk in range(2):
                wga = wtile(winT[3 + blk])
                wak = wtile(waT[blk])
                wgb = wtile(winT[5 + blk])
                wbk = wtile(wbT[0])
                for oc in range(4):
                    dc = blk * 4 + oc
                    b = proj(wga, oc, hnTt, 8)
                    ACT(tmpm[0].t, b.t[:, 0:N], AF.Sigmoid, [b, pvt], [tmpm[0]], bias=pcol(R_BCM + 12 + dc))
                    b = proj(wak, oc, yaT_t, 8)
                    TT("dve", tmpm[0].t, tmpm[0].t, b.t[:, 0:N], ALU.mult, [tmpm[0], b], [tmpm[0]])
                    b = proj(wgb, oc, hnTt, 8)
                    ACT(tmpm[1].t, b.t[:, 0:N], AF.Sigmoid, [b, pvt], [tmpm[1]], bias=pcol(R_BCM + 20 + dc))
                    b = bigp.next()
                    for kc in range(4):
                        MM(b.t[:, 0:N], wbk.t[:, kc, dc * 128:(dc + 1) * 128], ybT_t.t[:, kc, :], kc == 0, kc == 3, [wbk, ybT_t], [b])
                    TT("dve", tmpm[1].t, tmpm[1].t, b.t[:, 0:N], ALU.mult, [tmpm[1], b], [tmpm[1]])
                    TT("pool", mergedT.t[:, dc, :], tmpm[0].t, tmpm[1].t, ALU.add, [tmpm[0], tmpm[1]], [mergedT])
            for blk in range(2):
                wok = wtile(woT[blk])
                for oc in range(4):
                    dc = blk * 4 + oc
                    b = proj(wok, oc, mergedT, 8)
                    TT("dve", xTt[dc].t, xTt[dc].t, b.t[:, 0:N], ALU.add, [xTt[dc], b], [xTt[dc]])
            normT(R_G2, lambda dc: (hnTt.t[:, dc, :], hnTt))
            for hb_ in range(6):
                nh = 4 if hb_ < 5 else 2
                wgk = wtile(wgT[hb_])
                wuk = wtile(wuT[hb_])
                wdk = wtile(wd2T[hb_])
                ab = actb[hb_ % 2]
                for oc in range(nh):
                    bg = proj(wgk, oc, hnTt, 8)
                    bu = proj(wuk, oc, hnTt, 8)
                    ACT(tmpm[oc % 2].t, bg.t[:, 0:N], AF.Silu, [bg], [tmpm[oc % 2]])
                    TT("dve", ab.t[:, oc, :], tmpm[oc % 2].t, bu.t[:, 0:N], ALU.mult, [tmpm[oc % 2], bu], [ab])
                for dc in range(8):
                    b = bigp.next()
                    for oc in range(nh):
                        MM(b.t[:, 0:N], wdk.t[:, oc, dc * 128:(dc + 1) * 128], ab.t[:, oc, :], oc == 0, oc == nh - 1, [wdk, ab], [b])
                    TT("dve", xTt[dc].t, xTt[dc].t, b.t[:, 0:N], ALU.add, [xTt[dc], b], [xTt[dc]])
            normT(R_GF, lambda dc: (xTt[dc].t, xTt[dc]))
            for j in range(OB):
                for hf in range(2):
                    b = bigp.next()
                    for c in range(4):
                        dc = hf * 4 + c
                        TR(b.t[:, c * 128:(c + 1) * 128], xTt[dc].t[:, j * 128:(j + 1) * 128], identf, [xTt[dc], cf], [b])
                    CP("act", ot.t[:, hf * 512:(hf + 1) * 512], b.t[:, 0:512], [b], [ot])
                DMA("pool", y_d[(g0 + j) * 128:(g0 + j + 1) * 128, :], ot.t, [ot], [], ot.bs[0])

        def nq_part(g):
            wn = wtile(winT[2])
            for oc in range(4):
                bk = mmp.next()
                for kc in range(8):
                    MM(bk.t[:, 0:128], wn.t[:, kc, oc * 128:(oc + 1) * 128], hnT.t[:, kc, 128:256], kc == 0, kc == 7, [wn, hnc[kc]], [bk])
                ACT(nqT.t[:, oc, :], bk.t[:, 0:128], AF.Identity, [bk, pvt], [nqT], bias=pcol(R_BCM + 8 + oc))

        def B_part(g):
            tiles = [g * TPG + ti for ti in range(TPG)]
            gbr = gbrs[g % 2]
            prep_x(g, g + 1 if g + 1 < NG else None, (g % OB) * 128)

            wsm = wtile(winT[12])
            bi = mlp.next()
            for kc in range(8):
                MM(bi.t[0:4, 0:GT], wsm.t[:, kc, 280:284], hnT.t[:, kc, :], kc == 0, kc == 7, [wsm, hnc[kc]], [bi])
            bff = mlp.next()
            for kc in range(8):
                MM(bff.t[0:4, 0:GT], wsm.t[:, kc, 284:288], hnT.t[:, kc, :], kc == 0, kc == 7, [wsm, hnc[kc]], [bff])
            ACT(gA.t[0:4, 0:GT], bi.t[0:4, 0:GT], AF.Identity, [bi, pvt], [gA], bias=pvt.t[0:4, R_BI:R_BI + 1])
            ACT(gL.t[0:4, 0:GT], bff.t[0:4, 0:GT], AF.Exp, [bff, negbf], [gL], scale=-1.0, bias=negbf.t[0:4, 0:1])
            ACT(gL.t[0:4, 0:GT], gL.t[0:4, 0:GT], AF.Ln, [gL], [gL], bias=1.0)
            TS("dve", gL.t[0:4, 0:GT], gL.t[0:4, 0:GT], -1.0, None, ALU.mult, None, [gL], [gL])
            if g == 0:
                TS("dve", gL.t[0:4, 0:128], gL.t[0:4, 0:128], pcs.t[0:4, 0:1], None, ALU.mult, None, [gL, pcs], [gL])
                TS("dve", gA.t[0:4, 0:128], gA.t[0:4, 0:128], pcs.t[0:4, 0:1], pcs.t[0:4, 1:2], ALU.mult, ALU.add, [gA, pcs], [gA])
            P.op("dve", lambda e: e.tensor_tensor_scan(out=gB.t[0:4, 0:GT], data0=cf.t[0:4, F_ONES:F_ONES + GT], data1=gL.t[0:4, 0:GT],
                                                       initial=Bcar.t[0:4, 0:1], op0=ALU.mult, op1=ALU.add), [cf, gL, Bcar], [gB])
            TT("dve", gA.t[0:4, 0:GT], gA.t[0:4, 0:GT], gB.t[0:4, 0:GT], ALU.subtract, [gA, gB], [gA])
            P.op("dve", lambda e: e.tensor_tensor_scan(out=gG.t[0:4, 0:GT], data0=cf.t[0:4, F_ONES:F_ONES + GT], data1=gA.t[0:4, 0:GT],
                                                       initial=Gcar.t[0:4, 0:1], op0=ALU.mult, op1=ALU.max), [cf, gA, Gcar], [gG])
            TS("dve", gN.t[0:4, 0:GT], gG.t[0:4, 0:GT], -1.0, None, ALU.mult, None, [gG], [gN])
            for ti in range(TPG):
                c0 = ti * 128
                mu = Gcar.t[0:4, 0:1] if ti == 0 else gG.t[0:4, c0 - 1:c0]
                nmu = NGcar.t[0:4, 0:1] if ti == 0 else gN.t[0:4, c0 - 1:c0]
                ACT(gek.t[0:4, c0:c0 + 128], gA.t[0:4, c0:c0 + 128], AF.Exp, [gA, gN, NGcar], [gek], bias=nmu)
                ACT(geq.t[0:4, c0:c0 + 128], gG.t[0:4, c0:c0 + 128], AF.Exp, [gG, Gcar], [geq], scale=-1.0, bias=mu)
            TT("dve", gL.t[0:4, 0:GT], gB.t[0:4, 0:GT], gG.t[0:4, 0:GT], ALU.add, [gB, gG], [gL])
            ACT(gem.t[0:4, 0:GT], gL.t[0:4, 0:GT], AF.Exp, [gL], [gem], scale=-1.0)
            CP("dve", Bcar.t[0:4, 0:1], gB.t[0:4, GT - 1:GT], [gB], [Bcar])
            CP("dve", Gcar.t[0:4, 0:1], gG.t[0:4, GT - 1:GT], [gG], [Gcar])
            CP("dve", NGcar.t[0:4, 0:1], gN.t[0:4, GT - 1:GT], [gN], [NGcar])
            for ti in range(TPG):
                c0 = ti * 128
                bk = mlp.next()
                TR(bk.t[:, 0:4], gek.t[0:4, c0:c0 + 128], cf.t[0:4, 0:4], [gek, cf], [bk])
                TR(bk.t[:, 4:8], geq.t[0:4, c0:c0 + 128], cf.t[0:4, 0:4], [geq, cf], [bk])
                TR(bk.t[:, 8:12], gem.t[0:4, c0:c0 + 128], cf.t[0:4, 0:4], [gem, cf], [bk])
                CP("dve", scT.t[:, ti, :], bk.t[:, 0:12], [bk], [scT])
                TS("dve", scT.t[:, ti, 0:4], scT.t[:, ti, 0:4], float(0.25 * 128 ** -0.5), None, ALU.mult, None, [scT], [scT])
            CP("dve", dec4.t[:, :], geq.t[0:4, 127:GT:128], [geq], [dec4])
            TT("dve", rdg.t[:, :].rearrange("p (a b) -> p a b", a=TPG), dec4.t[:, :].unsqueeze(2).to_broadcast([4, TPG, 4]),
               cf.t[0:4, 0:4].unsqueeze(1).to_broadcast([4, TPG, 4]), ALU.mult, [dec4, cf], [rdg])
            bk = mlp.next()
            MM(bk.t[:, 0:TPG * 4], cf.t[0:4, F_ONES:F_ONES + 128], rdg.t[:, :], True, True, [cf, rdg], [bk])
            CP("dve", decbc.t[:, :], bk.t[:, 0:TPG * 4], [bk], [decbc])

            for ti in range(TPG):
                jq = tiles[ti]
                slot = jq % 8
                for which in range(3):
                    if which == 2 and ti == 0:
                        continue
                    c0, n, boff = [(0, 128, 2048), (128, 128, 2176), (256, 24, 2304)][which]
                    bk = mmp.next()
                    for kc in range(8):
                        MM(bk.t[:, 0:n], hnT.t[:, kc, ti * 128:(ti + 1) * 128], wsm.t[:, kc, c0:c0 + n], kc == 0, False, [wsm, hnc[kc]], [bk])
                    bias_mm(bk, n, boff)
                    if which == 0:
                        CP("dve", vslc.t[:, jq, :, 0:64], bk.t[:, 0:128].rearrange("p (a b) -> p a b", a=2), [bk], [vslc_b[jq]])
                    elif which == 1:
                        CP("dve", vwin.t[:, slot, :, 0:64], bk.t[:, 0:128].rearrange("p (a b) -> p a b", a=2), [bk], [vwin_b[slot]])
                    else:
                        ACT(gbr.t[:, ti, :], bk.t[:, 0:24], AF.Tanh, [bk], [gbr], scale=0.5)
                        TS("dve", gbr.t[:, ti, :], gbr.t[:, ti, :], 1.0, 0.5, ALU.add, ALU.mult, [gbr], [gbr])

            for hf in range(2):
                wv = wtile(winT[8 + hf])
                for ti in range(TPG):
                    bk = mmp.next()
                    for kc in range(8):
                        MM(bk.t[:, 0:512], hnT.t[:, kc, ti * 128:(ti + 1) * 128], wv.t[:, kc, :], kc == 0, False, [wv, hnc[kc]], [bk])
                    bias_mm(bk, 512, hf * 512)
                    for hh in range(2):
                        h = hf * 2 + hh
                        ACT(vt.t[:, ti, h, 0:256], bk.t[:, hh * 256:(hh + 1) * 256], AF.Copy, [bk, scT], [vt], scale=scT.t[:, ti, h:h + 1])
            for ti in range(TPG):
                CP("dve", vt.t[:, ti, :, 256], scT.t[:, ti, 0:4], [scT], [vt])
            for hf in range(2):
                wo_ = wtile(winT[10 + hf])
                for ti in (1,):
                    bk = mmp.next()
                    for kc in range(8):
                        MM(bk.t[:, 0:512], hnT.t[:, kc, ti * 128:(ti + 1) * 128], wo_.t[:, kc, :], kc == 0, False, [wo_, hnc[kc]], [bk])
                    bias_mm(bk, 512, 1024 + hf * 512)
                    ACT(sigo.t[:, hf * 512:(hf + 1) * 512], bk.t[:, 0:512], AF.Tanh, [bk], [sigo], scale=0.5)
                    STT("dve", sigo.t[:, hf * 512:(hf + 1) * 512], sigo.t[:, hf * 512:(hf + 1) * 512], 1.0, gmbc.t[:, hf * 512:(hf + 1) * 512], ALU.add, ALU.mult, [sigo, gmbc], [sigo])

            def cm_chunk(wblk, oc, lo=0, hi=GT):
                bk = mmp.next()
                for kc in range(8):
                    MM(bk.t[:, 0:hi - lo], wblk.t[:, kc, oc * 128:(oc + 1) * 128], hnT.t[:, kc, lo:hi], kc == 0, kc == 7, [wblk, hnc[kc]], [bk])
                return bk

            wqs = {}

            def conv_s1(c):
                blk, oc = divmod(c, 4)
                if blk not in wqs:
                    wqs[blk] = wtile(winT[blk])
                bk = cm_chunk(wqs[blk], oc)
                pr = pre[c % 2]
                CP("pool", pr.t[:, 0:3], halo.t[:, c, :], [halo], [pr])
                ACT(pr.t[:, 3:3 + GT], bk.t[:, 0:GT], AF.Identity, [bk, pvt], [pr], bias=pcol(R_BCM + c))
                CP("pool", halo.t[:, c, :], pr.t[:, GT:GT + 3], [pr], [halo])

            def conv_s2(c):
                pr = pre[c % 2]
                cv = cvt[c % 2]
                if g == 0:
                    TS("dve", pr.t[:, 3:131], pr.t[:, 3:131], pcs.t[:, 0:1], None, ALU.mult, None, [pr, pcs], [pr])
                TS("dve", cv.t, pr.t[:, 0:GT], pcol(R_CW + c), pcol(R_CB + c), ALU.mult, ALU.add, [pr, pvt], [cv])
                for j in range(1, 4):
                    STT("dve", cv.t, pr.t[:, j:j + GT], pcol(R_CW + j * 8 + c), cv.t, ALU.mult, ALU.add, [pr, pvt, cv], [cv])
                ACT(pr.t[:, 0:GT], cv.t, AF.Tanh, [cv], [pr], scale=0.5)
                TT("pool", pr.t[:, 0:GT], pr.t[:, 0:GT], cv.t, ALU.mult, [pr, cv], [pr])
                TT("pool", qkT.t[:, c, :], pr.t[:, 0:GT], cv.t, ALU.add, [pr, cv], [qkT])

            conv_s1(0)
            for c in range(8):
                if c + 1 < 8:
                    conv_s1(c + 1)
                conv_s2(c)
            wk = wtile(winT[7])
            bk = cm_chunk(wk, 0)
            ACT(kslcT.t[:, g * GT:(g + 1) * GT], bk.t[:, 0:GT], AF.Identity, [bk, pvt], [kslc_b[j] for j in tiles], bias=pcol(R_BCM + 28))
            bk = cm_chunk(wk, 1)
            s0 = tiles[0] % 8
            ACT(kwinT.t[:, s0 * 128:(s0 + TPG) * 128], bk.t[:, 0:GT], AF.Identity, [bk, pvt], [kwin_b[j % 8] for j in tiles], bias=pcol(R_BCM + 29))

            for ti in range(TPG):
                c0 = ti * 128
                for h in range(4):
                    par = (ti * 4 + h) % 2
                    kT_ap = qkT.t[:, 4 + h, c0:c0 + 128]
                    qT_ap = qkT.t[:, h, c0:c0 + 128]
                    b1 = mlp.next()
                    TR(bfv(b1)[:, 0:128], kT_ap, identb, [qkT, cb], [b1])
                    CP("dve", ktok[par].t[:], bfv(b1)[:, 0:128], [b1], [ktok[par]])
                    if ti == 1:
                      b2 = mlp.next()
                      MM(b2.t[:, 0:128], kT_ap, qT_ap, True, True, [qkT], [b2])
                      TT("dve", stm[par].t[:], b2.t[:, 0:128], tri, ALU.mult, [b2, cf], [stm[par]])
                      b3 = mlp.next()
                      MM(b3.t[:, 0:257], stm[par].t[:], vt.t[:, ti, h, 0:257], True, False, [stm[par], vt], [b3])
                      MM(b3.t[:, 0:257], qT_ap, Cbf.t[:, h, 0:257], False, True, [qkT, cbf_b[h]], [b3])
                      t1 = t1s[par]
                      eq_h = scT.t[:, ti, 4 + h:5 + h]
                      em_h = scT.t[:, ti, 8 + h:9 + h]
                      ACT(t1.t[:, 0:1], b3.t[:, 256:257], AF.Abs, [b3, scT], [t1], scale=eq_h)
                      TS("dve", t1.t[:, 0:1], t1.t[:, 0:1], em_h, None, ALU.max, None, [t1, scT], [t1])
                      P.op("dve", lambda e, t1=t1: e.reciprocal(out=t1.t[:, 3:4], in_=t1.t[:, 0:1]), [t1], [t1])
                      TS("dve", t1.t[:, 1:2], t1.t[:, 3:4], eq_h, None, ALU.mult, None, [t1, scT], [t1])
                      ACT(hb[par].t, b3.t[:, 0:256], AF.Copy, [b3, t1], [hb[par]], scale=t1.t[:, 1:2])
                      ACT(junk.t[:, :], hb[par].t, AF.Square, [hb[par]], [junk, ssq[par]], accum=ssq[par].t[:, 0:1])
                      ACT(t1.t[:, 2:3], ssq[par].t[:, 0:1], AF.Ln, [ssq[par]], [t1], scale=1.0 / 256, bias=1e-6)
                      ACT(t1.t[:, 2:3], t1.t[:, 2:3], AF.Exp, [t1], [t1], scale=-0.5)
                      STT("dve", ya.t[:, h * 256:(h + 1) * 256], hb[par].t, t1.t[:, 2:3], sigo.t[:, h * 256:(h + 1) * 256],
                          ALU.mult, ALU.mult, [hb[par], t1, sigo], [ya])
                    b4 = mlp.next()
                    MM(b4.t[:, 0:257], ktok[par].t[:], vt.t[:, ti, h, 0:257], True, True, [ktok[par], vt], [b4])
                    dcol = decbc.t[:, ti * 4 + h:ti * 4 + h + 1]
                    ACT(dtmp[par].t, b4.t[:, 0:257], AF.Copy, [b4, decbc], [dtmp[par]], scale=dcol)
                    STT("dve", stateF.t[:, h, :], stateF.t[:, h, :], dcol, dtmp[par].t, ALU.mult, ALU.add, [state_b[h], decbc, dtmp[par]], [state_b[h]])
                    CP("pool", Cbf.t[:, h, 0:257], stateF.t[:, h, :], [state_b[h]], [cbf_b[h]])
                if dbg and ti == 1:
                    DMA("pool", dya_d[g * 128:(g + 1) * 128, :], ya.t[:, :], [ya], [], ya.bs[0])

        def A_part(g):
            tiles = [g * TPG + ti for ti in range(TPG)]
            gbr = gbrs[g % 2]
            def rhs_q(pb, c0):
                return nqT.t[pb:pb + 64, :, :]

            def bc4(ap):
                return ap.unsqueeze(1).to_broadcast([ap.shape[0], 4, 128])

            def out4(bank):
                return bank.t[:, 0:512].rearrange("p (a b) -> p a b", a=4)

            def fin_cmp(ti, jq, gi):
                sl0 = 128 - 2 * jq
                pb = 64 * gi
                for bi_, bank in enumerate((ocA, ocB)):
                    TS("dve", rs4.t[:, 2 * bi_:2 * bi_ + 2], bank.t[:, 64:512:256], 1e-30, None, ALU.max, None, [bank], [rs4])
                P.op("dve", lambda e: e.reciprocal(out=rs4.t[:, :], in_=rs4.t[:, :]), [rs4], [rs4])
                TS("dve", impb.t, ocA.t[:, 65:193], rs4.t[:, 0:1], None, ALU.mult, None, [ocA, rs4], [impb])
                STT("dve", impb.t, ocA.t[:, 256 + 65:256 + 193], rs4.t[:, 1:2], impb.t, ALU.mult, ALU.add, [ocA, rs4, impb], [impb])
                STT("dve", impb.t, ocB.t[:, 65:193], rs4.t[:, 2:3], impb.t, ALU.mult, ALU.add, [ocB, rs4, impb], [impb])
                STT("dve", impb.t, ocB.t[:, 256 + 65:256 + 193], rs4.t[:, 3:4], impb.t, ALU.mult, ALU.add, [ocB, rs4, impb], [impb])
                TT("dve", w4.t[:, :], rs4.t[:, :], gbr.t[:, ti, gi * 12 + 0:gi * 12 + 12:3], ALU.mult, [rs4, gbr], [w4])
                for bi_, bank in enumerate((ocA, ocB)):
                    TT("dve", yb.t[:, gi * 256 + bi_ * 128:gi * 256 + (bi_ + 1) * 128].rearrange("p (a b) -> p a b", a=2),
                       bank.t[:, 0:512].rearrange("p (a b) -> p a b", a=2)[:, :, 0:64],
                       w4.t[:, 2 * bi_:2 * bi_ + 2].unsqueeze(2).to_broadcast([128, 2, 64]), ALU.mult, [bank, w4], [yb])
                TT("dve", scb.t, impb.t, cf.t[:, F_PMUL + sl0:F_PMUL + sl0 + 128], ALU.mult, [impb, cf], [scb])
                TT("dve", scb.t, scb.t, cf.t[:, F_PADD + sl0:F_PADD + sl0 + 128], ALU.add, [scb, cf], [scb])
                TT("dve", scb.t, scb.t, pcs.t[:, 4:132], ALU.max, [scb, pcs], [scb])
                TT("dve", scb.t, scb.t, pcs.t[:, 132:260], ALU.min, [scb, pcs], [scb])
                P.op("dve", lambda e: e.max(out=m8.t[:, 0:8], in_=scb.t), [scb], [m8])
                P.op("dve", lambda e: e.match_replace(out=sc2b.t, in_to_replace=m8.t[:, 0:8], in_values=scb.t, imm_value=-3e9), [scb, m8], [sc2b])
                P.op("dve", lambda e: e.max(out=m8.t[:, 8:16], in_=sc2b.t), [sc2b], [m8])
                TS("dve", selb.t, scb.t, m8.t[:, 15:16], None, ALU.is_ge, None, [scb, m8], [selb])
                TT("dve", selb.t, selb.t, cf.t[:, F_PELIG + sl0:F_PELIG + sl0 + 128], ALU.mult, [selb, cf], [selb])
                TS("dve", selbb.t[:, :], selb.t, BIG, -BIG, ALU.mult, ALU.add, [selb], [selbb])
                bt = atp.next()
                TR(bfv(bt)[pb:pb + 64, 0:128], selbb.t[:, 0:64], identb, [selbb, cb], [bt])
                TR(bfv(bt)[pb:pb + 64, 128:256], selbb.t[:, 64:128], identb, [selbb, cb], [bt])
                CP("dve", negselT[gi].t[pb:pb + 64, :], bfv(bt)[pb:pb + 64, 0:256], [bt], [negselT[gi]])

            def fin_att(ti, jq, gi, br, bank):
                TS("dve", rs4.t[:, :], bank.t[:, 64:512:128], 1e-30, None, ALU.max, None, [bank], [rs4])
                P.op("dve", lambda e: e.reciprocal(out=rs4.t[:, :], in_=rs4.t[:, :]), [rs4], [rs4])
                TT("dve", w4.t[:, :], rs4.t[:, :], gbr.t[:, ti, gi * 12 + br:gi * 12 + 12:3], ALU.mult, [rs4, gbr], [w4])
                TT("dve", ybtmp.t[:, :].rearrange("p (a b) -> p a b", a=4),
                   bank.t[:, 0:512].rearrange("p (a b) -> p a b", a=4)[:, :, 0:64],
                   w4.t[:, :].unsqueeze(2).to_broadcast([128, 4, 64]), ALU.mult, [bank, w4], [ybtmp])
                TT("dve", yb.t[:, gi * 256:(gi + 1) * 256], yb.t[:, gi * 256:(gi + 1) * 256], ybtmp.t[:, :], ALU.add, [yb, ybtmp], [yb])

            def cmp_item(ti, jq, gi, ct):
                c0 = ti * 128
                pb = 64 * gi
                ctm = jq // 16
                mpat = jq % 16
                stt = {}

                def A():
                    bs_ = atp.next()
                    masked = (ct == ctm) or (ct == ctm - 1 and mpat == 0)
                    MM(out4(bs_), kcT.t[pb:pb + 64, ct * 128:(ct + 1) * 128], rhs_q(pb, c0), True, not masked, [kcT, nqT], [bs_])
                    if ct == ctm:
                        MM(out4(bs_), identb, bc4(cb.t[:, B_MC + mpat * 128:B_MC + (mpat + 1) * 128]), False, True, [cb], [bs_])
                    elif masked:
                        MM(out4(bs_), identb, bc4(cb.t[:, B_MC + 16 * 128:B_MC + 17 * 128]), False, True, [cb], [bs_])
                    pt = ptp.next()
                    if ct == 0:
                        ACT(pt.t, bs_.t[:, 0:512], AF.Exp, [bs_, pcs], [pt], scale=0.125, bias=pcs.t[:, 3:4])
                    else:
                        ACT(pt.t, bs_.t[:, 0:512], AF.Exp, [bs_], [pt], scale=0.125)
                    stt["pt"] = pt

                def B():
                    pt = stt["pt"]
                    for hh in range(4):
                        bank = ocA if hh < 2 else ocB
                        o0 = (hh % 2) * 256
                        MM(bank.t[:, o0:o0 + 193], pt.t[:, hh * 128:(hh + 1) * 128], vcx.t[:, ct, gi, :],
                           ct == 0 and hh % 2 == 0, ct == ctm and hh % 2 == 1, [pt, vcx], [bank], skip=True)
                    if ct == ctm:
                        fin_cmp(ti, jq, gi)
                return (A, B)

            def att_pair(ti, jq, br, kt, kts):
                c0 = ti * 128
                accs = (ocA, ocB)
                stt = {}

                def A():
                    bsl = [atp.next(), atp.next()]
                    if br == 2:
                        slot = kt % 8
                        masked = (kt == jq) or (kt == jq - 4)
                        for gi in range(2):
                            pb = 64 * gi
                            MM(out4(bsl[gi]), kwinT.t[pb:pb + 64, slot * 128:(slot + 1) * 128], rhs_q(pb, c0), True, not masked, [kwin_b[slot], nqT], [bsl[gi]])
                        if masked:
                            mcol = B_TN if kt == jq else B_TN2
                            for gi in range(2):
                                MM(out4(bsl[gi]), identb, bc4(cb.t[:, mcol:mcol + 128]), False, True, [cb], [bsl[gi]])
                    else:
                        q64 = kt // 32
                        for gi in range(2):
                            pb = 64 * gi
                            MM(out4(bsl[gi]), kslcT.t[pb:pb + 64, kt * 128:(kt + 1) * 128], rhs_q(pb, c0), True, False, [kslc_b[kt], nqT], [bsl[gi]])
                        for gi in range(2):
                            pb = 64 * gi
                            MM(out4(bsl[gi]), cb.t[pb:pb + 64, B_W + (kt % 32) * 128:B_W + (kt % 32 + 1) * 128],
                               bc4(negselT[gi].t[pb:pb + 64, q64 * 128:(q64 + 1) * 128]), False, kt != jq, [cb, negselT[gi]], [bsl[gi]])
                        if kt == jq:
                            for gi in range(2):
                                MM(out4(bsl[gi]), identb, bc4(cb.t[:, B_TN:B_TN + 128]), False, True, [cb], [bsl[gi]])
                    pts = []
                    for gi in range(2):
                        pt = ptp.next()
                        if kt == 0:
                            ACT(pt.t, bsl[gi].t[:, 0:512], AF.Exp, [bsl[gi], pcs], [pt], scale=0.125, bias=pcs.t[:, 2:3])
                        else:
                            ACT(pt.t, bsl[gi].t[:, 0:512], AF.Exp, [bsl[gi]], [pt], scale=0.125)
                        pts.append(pt)
                    stt["pts"] = pts

                def B():
                    for gi in range(2):
                        pt = stt["pts"][gi]
                        bank = accs[gi]
                        if br == 2:
                            slot = kt % 8
                            vrhs = vwin.t[:, slot, gi, :]
                            vdep = vwin_b[slot]
                        else:
                            vrhs = vslc.t[:, kt, gi, :]
                            vdep = vslc_b[kt]
                        for hh in range(4):
                            MM(bank.t[:, hh * 128:hh * 128 + 65], pt.t[:, hh * 128:(hh + 1) * 128], vrhs,
                               kt == kts[0] and hh == 0, kt == kts[-1] and hh == 3, [pt, vdep], [bank], skip=True)
                    if kt == kts[-1]:
                        for gi in range(2):
                            fin_att(ti, jq, gi, br, accs[gi])
                return (A, B)

            for ti in (1,):
                jq = tiles[ti]
                items = []
                for gi in range(2):
                    for ct in range(jq // 16 + 1):
                        items.append(cmp_item(ti, jq, gi, ct))
                kts = list(range(max(0, jq - 4), jq + 1))
                for kt in kts:
                    items.append(att_pair(ti, jq, 2, kt, kts))
                kts = list(range(0, jq + 1))
                for kt in kts:
                    items.append(att_pair(ti, jq, 1, kt, kts))
                items[0][0]()
                for i in range(len(items)):
                    if i + 1 < len(items):
                        items[i + 1][0]()
                    items[i][1]()
                if dbg:
                    DMA("pool", dyb_d[g * 128:(g + 1) * 128, :], yb.t[:, :], [yb], [], yb.bs[0])

        def pools(merged):
            if merged:
                mmp.items = [banks[0], banks[1]]
                mlp.items = [banks[0], banks[1]]
            else:
                mmp.items = [banks[0], banks[1], banks[4], banks[5]]
                mlp.items = [banks[6], banks[7], banks[2], banks[3]]

        pools(False)
        B_part(0)
        nq_part(0)
        for g in range(NG):
            pools(True)
            if g % OB != OB - 1:
                ytr_a(g)
                la = P.capture(lambda: A_part(g))
                lb = P.capture(lambda: B_part(g + 1))
                P.replay(Prog.merge(la, lb))
                nq_part(g + 1)
                ytr_b(g)
            else:
                A_part(g)
                ytr_a(g)
                ytr_b(g)
                dense_tail(g - OB + 1)
                if g + 1 < NG:
                    pools(False)
                    B_part(g + 1)
                    nq_part(g + 1)
        P.fence("pool", [ot, ya, yb])
        P.fence("sp", [ot])
        print("ops:", {e: len(P.ops[e]) for e in ENGS}, "sems:", len(P.sems), "sbuf left:", nc.sbuf_bytes_remaining)
        P.emit()
    return nc


_CACHE = {}


def prep_inputs(inp, S):
    perm = make_perm()
    w_in = np.ascontiguousarray(inp["w_in"][0][:, perm])
    b = inp["b_in"][0][perm]
    pv = np.zeros((128, 128), np.float32)
    pv[R_BCM:R_BCM + 32] = b[0:4096].reshape(32, 128)
    pv[R_G1:R_G1 + 8] = inp["norm1_g"][0].reshape(8, 128)
    pv[R_G2:R_G2 + 8] = inp["norm2_g"][0].reshape(8, 128)
    pv[R_GF:R_GF + 8] = inp["norm_f_g"].reshape(8, 128)
    pv[R_CW:R_CW + 32] = inp["conv_w"][0].reshape(4, 8, 128).reshape(32, 128)
    pv[R_CB:R_CB + 8] = inp["conv_b"][0].reshape(8, 128)
    pv[R_BI, 0:4] = b[C_I:C_I + 4]
    pv[R_BF, 0:4] = b[C_F:C_F + 4]
    pv[R_FB, 0:4] = inp["f_bias"][0]
    cf, cb, mm = make_consts(S)
    com = {
        "w_in": w_in, "pv": pv, "brow": np.ascontiguousarray(b[4096:6424].reshape(1, 2328)),
        "gm": np.ascontiguousarray(inp["mlstm_norm_g"][0].reshape(1, 1024)),
        "ckw1": inp["cmp_k_w1"][0], "ckw2": inp["cmp_k_w2"][0], "cvw1": inp["cmp_v_w1"][0], "cvw2": inp["cmp_v_w2"][0],
        "ckposT": np.ascontiguousarray(inp["cmp_k_pos"][0].T), "cvposT": np.ascontiguousarray(inp["cmp_v_pos"][0].T),
        "wa": inp["w_branch_a"][0], "wb": inp["w_branch_b"][0], "wo": inp["w_out"][0],
        "wg": inp["w_ffn_gate"][0], "wu": inp["w_ffn_up"][0], "wd": inp["w_ffn_down"][0],
        "constf": cf, "constb": cb, "mmap": mm,
    }
    return {k: np.ascontiguousarray(v) for k, v in com.items()}


def percore(t):
    pc = np.zeros((128, 260), np.float32)
    pc[:, 0] = float(t)
    pc[:, 1] = (float(t) - 1.0) * 200.0
    pc[:, 2] = 0.0 if t == 1 else -BIG
    if t == 0:
        pc[0:8, 3] = -BIG
    f0 = np.full((128, 128), -3e9, np.float32)
    f0[:, 0 if t == 1 else 2] = 1e9
    v = np.full((128, 128), 3e9, np.float32)
    if t == 0:
        v[:, 0:2] = -1e9
    pc[:, 4:132] = f0
    pc[:, 132:260] = v
    return pc


def local_x(xb, t):
    if t == 1:
        return np.ascontiguousarray(xb)
    return np.ascontiguousarray(np.concatenate([np.zeros((128, xb.shape[1]), xb.dtype), xb[:-128]], axis=0))


def kernel(**inputs):
    inp = {k: np.asarray(v) for k, v in inputs.items()}
    x = inp["x"]
    B, S, _ = x.shape
    if S not in _CACHE:
        _CACHE[S] = build(S)
    nc = _CACHE[S]
    com = prep_inputs(inp, S)
    n = 2 * B
    in_maps = []
    for c in range(n):
        m = dict(com)
        m["x"] = local_x(x[c // 2], c % 2)
        m["pc"] = percore(c % 2)
        in_maps.append(m)
    res = run_bass_kernel_spmd(nc, in_maps, core_ids=list(range(n)))
    out = np.empty((B, S // 128, 128, x.shape[2]), np.float32)
    for c in range(n):
        y = np.asarray(res.results[c]["y"]).reshape(S // 256, 128, x.shape[2])
        out[c // 2, (c % 2)::2] = y
    return out.reshape(B, S, x.shape[2])
```

```python
import numpy as np
import ml_dtypes
import concourse.bass as bass
import concourse.mybir as mybir
from concourse.bass_utils import run_bass_kernel_spmd
from contextlib import ExitStack

F32 = mybir.dt.float32
BF16 = mybir.dt.bfloat16
AF = mybir.ActivationFunctionType
ALU = mybir.AluOpType

ENGS = ("pe", "act", "dve", "pool", "sp")
SAME_SYNC = True
SEM_EPOCH = 16000

TPG = 2
GT = 128 * TPG
BIG = 30000.0
D = 1024
FF = 2816
NHC = 22
IN_W = 6432

C_MQ, C_MK, C_NQ, C_GA, C_GB, C_KSLC, C_KWIN, C_KCMP, C_VCMP = 0, 512, 1024, 1536, 2560, 3584, 3712, 3840, 3968
C_V, C_O, C_VSLC, C_VWIN, C_NG, C_I, C_F = 4096, 5120, 6144, 6272, 6400, 6424, 6428
R_BCM, R_G1, R_G2, R_GF, R_CW, R_CB, R_BI, R_BF, R_FB = 0, 32, 40, 48, 56, 88, 96, 97, 98
F_ID, F_PMUL, F_PADD, F_PELIG, F_TRI, F_ONES, NF = 0, 128, 384, 640, 896, 1024, 1536
B_ID, B_TN, B_TN2, B_MC, B_W, B_ONES, NB = 0, 128, 256, 384, 384 + 17 * 128, 384 + 17 * 128 + 4096, 384 + 17 * 128 + 4096 + 128


class Buf:
    __slots__ = ("name", "w", "rs", "semkey", "semval")

    def __init__(self, name):
        self.name = name
        self.w = None
        self.rs = {}
        self.semkey = None
        self.semval = 0


class TB:
    __slots__ = ("t", "bs")

    def __init__(self, t, bs):
        self.t = t
        self.bs = bs


def _flat(tbs):
    out = []
    for x in tbs:
        if isinstance(x, TB):
            out.extend(x.bs)
        elif isinstance(x, Buf):
            out.append(x)
        else:
            out.extend(_flat(x))
    return out


class Prog:
    def __init__(self, nc, stack):
        self.nc = nc
        self.stack = stack
        self.ops = {e: [] for e in ENGS}
        self.cnt = {e: 0 for e in ENGS}
        self.waited = {e: {} for e in ENGS}
        self.sems = {}
        for e in ENGS:
            self.sems[e] = stack.enter_context(nc.semaphore("s_" + e))
        self.nbuf = 0
        self.cap = None

    def buf(self, name="b"):
        self.nbuf += 1
        return Buf("%s%d" % (name, self.nbuf))

    def sb(self, name, shape, dtype):
        t = self.stack.enter_context(self.nc.sbuf_tensor("sb_" + name, list(shape), dtype))
        return TB(t, [self.buf(name)])

    def ps(self, name, shape, dtype=F32):
        t = self.stack.enter_context(self.nc.psum_tensor("ps_" + name, list(shape), dtype))
        return TB(t, [self.buf(name)])

    def _record(self, eng, fn, reads, writes, token, inc):
        deps = {}

        def add(tok):
            if tok is None:
                return
            k, v = tok
            if deps.get(k, 0) < v:
                deps[k] = v
        for b in reads:
            add(b.w)
        for b in writes:
            add(b.w)
            for k, v in b.rs.items():
                add((k, v))
        waits = []
        wd = self.waited[eng]
        for k, v in deps.items():
            if k.split("#")[0] == eng and (eng == "pe" or not SAME_SYNC):
                continue
            if wd.get(k, 0) >= v:
                continue
            wd[k] = v
            waits.append((k, v))
        self.ops[eng].append((waits, fn, token, inc))
        if token is not None:
            for b in reads:
                if b.rs.get(token[0], 0) < token[1]:
                    b.rs[token[0]] = token[1]
            for b in writes:
                b.w = token
                b.rs = {}

    def capture(self, f):
        self.cap = []
        f()
        lst = self.cap
        self.cap = None
        return lst

    def replay(self, lst):
        for kind, eng, fn, reads, writes, sembuf, cost in lst:
            if kind == "op":
                self.op(eng, fn, reads, writes)
            else:
                self.dma(eng, fn, reads, writes, sembuf)

    @staticmethod
    def merge(a, b):
        LAT = 1.0
        eng_free = {}
        bw = {}
        br = {}
        out = []

        def ready(o):
            kind, eng, fn, reads, writes, sembuf, cost = o
            t = eng_free.get(eng, 0.0)
            for bf in _flat(reads):
                w = bw.get(id(bf))
                if w is not None:
                    t = max(t, w[0] + (LAT if w[1] != eng else 0.05))
            for bf in _flat(writes):
                w = bw.get(id(bf))
                if w is not None:
                    t = max(t, w[0] + (LAT if w[1] != eng else 0.05))
                r = br.get(id(bf))
                if r is not None:
                    t = max(t, r + LAT)
            return t

        def commit(o, t):
            kind, eng, fn, reads, writes, sembuf, cost = o
            if kind == "dma":
                eng_free[eng] = t + 0.1
                fin = t + 3.0
            else:
                fin = t + cost
                eng_free[eng] = fin
            for bf in _flat(reads):
                br[id(bf)] = max(br.get(id(bf), 0.0), fin)
            for bf in _flat(writes):
                bw[id(bf)] = (fin, eng)
                br.pop(id(bf), None)
            out.append(o)

        i = j = 0
        while i < len(a) or j < len(b):
            ta = ready(a[i]) if i < len(a) else float("inf")
            tb = ready(b[j]) if j < len(b) else float("inf")
            if ta <= tb:
                commit(a[i], ta)
                i += 1
            else:
                commit(b[j], tb)
                j += 1
        return out

    def op(self, eng, fn, reads=(), writes=(), cost=0.3):
        if self.cap is not None:
            self.cap.append(("op", eng, fn, reads, writes, None, cost))
            return
        c = self.cnt[eng]
        self.cnt[eng] += 1
        ep = c // SEM_EPOCH
        key = eng if ep == 0 else "%s#%d" % (eng, ep)
        if key not in self.sems:
            self.sems[key] = self.stack.enter_context(self.nc.semaphore("s_%s_%d" % (eng, ep)))
        token = (key, c % SEM_EPOCH + 1)
        self._record(eng, fn, _flat(reads), _flat(writes), token, 1)

    def dma(self, eng, fn, reads=(), writes=(), sembuf=None):
        if self.cap is not None:
            self.cap.append(("dma", eng, fn, reads, writes, sembuf, 3.0))
            return
        if sembuf.semkey is None:
            sembuf.semkey = "d_" + sembuf.name
            self.sems[sembuf.semkey] = self.stack.enter_context(self.nc.semaphore(sembuf.semkey))
        sembuf.semval += 16
        token = (sembuf.semkey, sembuf.semval)
        self._record(eng, fn, _flat(reads), _flat(writes), token, 16)

    def fence(self, eng, tbs):
        bs = _flat(tbs)
        self._record(eng, None, bs, bs, None, 0)

    def emit(self):
        nc = self.nc
        sems = self.sems
        ops = self.ops

        def run(name, e):
            for waits, fn, token, inc in ops[name]:
                for k, v in waits:
                    e.wait_ge(sems[k], v)
                if fn is not None:
                    ins = fn(e)
                    ins.then_inc(sems[token[0]], inc)

        with nc.Block() as block:
            @block.tensor
            def _(e):
                run("pe", e)

            @block.scalar
            def _(e):
                run("act", e)

            @block.vector
            def _(e):
                run("dve", e)

            @block.gpsimd
            def _(e):
                run("pool", e)

            @block.sync
            def _(e):
                run("sp", e)


class Arena:
    GR = 512

    def __init__(self, P, name, nbytes):
        self.P = P
        self.nbytes = nbytes
        self.t = P.stack.enter_context(P.nc.sbuf_tensor("sb_" + name, [128, nbytes // 2], BF16))
        self.g = [P.buf(name + "g") for _ in range((nbytes + self.GR - 1) // self.GR)]

    def view(self, off, shape, dtype):
        esz = 4 if dtype == F32 else 2
        n = int(np.prod(shape[1:]))
        nb = n * esz
        assert off % 4 == 0 and off + nb <= self.nbytes, (off, nb, self.nbytes)
        ap = self.t[:, off // 2:(off + nb) // 2]
        if dtype == F32:
            ap = ap.bitcast(F32)
        if len(shape) == 3:
            ap = ap.rearrange("p (a b) -> p a b", a=shape[1])
        elif len(shape) == 4:
            ap = ap.rearrange("p (a b c) -> p a b c", a=shape[1], b=shape[2])
        return TB(ap, self.g[off // self.GR:(off + nb - 1) // self.GR + 1])


class Bump:
    def __init__(self, arena, base):
        self.a = arena
        self.off = base

    def take(self, shape, dtype):
        esz = 4 if dtype == F32 else 2
        nb = int(np.prod(shape[1:])) * esz
        v = self.a.view(self.off, shape, dtype)
        self.off += (nb + Arena.GR - 1) // Arena.GR * Arena.GR
        return v


class Pool:
    def __init__(self, items):
        self.items = items
        self.i = 0

    def next(self):
        x = self.items[self.i % len(self.items)]
        self.i += 1
        return x


def make_perm():
    O_V, O_O, O_I, O_F, O_NQ, O_NKV, O_NG, O_GA, O_GB = 1024, 2048, 3072, 3076, 3080, 3592, 4360, 4384, 5408
    r = np.arange
    p = [r(0, 1024)]
    for i in range(4):
        p += [O_NQ + i * 64 + r(64), O_NQ + (4 + i) * 64 + r(64)]
    p += [O_GA + r(1024), O_GB + r(1024)]
    p += [O_NKV + 256 + r(128), O_NKV + 512 + r(128), O_NKV + r(128), O_NKV + 128 + r(128)]
    p += [O_V + r(1024), O_O + r(1024), O_NKV + 384 + r(128), O_NKV + 640 + r(128), O_NG + r(24), O_I + r(4), O_F + r(4)]
    p = np.concatenate(p)
    assert p.shape[0] == IN_W and len(set(p.tolist())) == IN_W
    return p


def make_consts(S):
    q = np.arange(128)
    cf = np.zeros((128, NF), np.float32)
    cf[:, F_ID:F_ID + 128] = np.eye(128)
    xx = np.arange(256)
    rel = (xx[None, :] - 128) - (q[:, None] // 64)
    cf[:, F_PMUL:F_PMUL + 256] = (rel <= -2)
    cf[:, F_PADD:F_PADD + 256] = np.where((rel == 0) | (rel == -1), 1e9, np.where(rel > 0, -1e9, 0.0))
    cf[:, F_PELIG:F_PELIG + 256] = (rel <= 0)
    cf[:, F_TRI:F_TRI + 128] = (q[:, None] <= q[None, :])
    cf[:, F_ONES:NF] = 1.0
    cb = np.zeros((128, NB), np.float32)
    cb[:, B_ID:B_ID + 128] = np.eye(128)
    cb[:, B_TN:B_TN + 128] = np.where(q[:, None] > q[None, :], -BIG, 0.0)
    cb[:, B_TN2:B_TN2 + 128] = np.where(q[:, None] <= q[None, :], -BIG, 0.0)
    for m in range(17):
        cb[:, B_MC + m * 128:B_MC + (m + 1) * 128] = np.where(16 * q[:, None] + 31 - 128 * m > q[None, :], -BIG, 0.0)
    y = np.arange(4096)
    cb[:, B_W:B_W + 4096] = ((q[:, None] % 64) == (y[None, :] // 64))
    cb[:, B_ONES:NB] = 1.0
    cpad = S // 16
    ncmp = cpad - 1
    c = np.arange(cpad)
    j = np.arange(128)
    dlt = c[:, None] - 4 * j[None, :]
    mm = np.where((dlt == -1) | (dlt == 3), 1.0, np.where((dlt >= 0) & (dlt <= 2), 2.0, 0.0))
    mm = mm * (c[:, None] < ncmp)
    nct = cpad // 128
    mm = mm.reshape(nct, 128, 128).transpose(1, 0, 2)
    return cf, cb.astype(ml_dtypes.bfloat16), np.ascontiguousarray(mm).astype(ml_dtypes.bfloat16)


def build(S, dbg=False):
    NT = S // 128
    NG = NT // TPG
    CPAD = S // 16
    NCMP = CPAD - 1
    NCT = CPAD // 128
    nc = bass.Bass("TRN2", target_bir_lowering=False)

    def din(name, shape, dt=F32):
        return nc.dram_tensor(name, list(shape), dt, kind="ExternalInput").ap()

    x_d = din("x", [S, D])
    win_d = din("w_in", [D, IN_W])
    pv_d = din("pv", [128, 128])
    brow_d = din("brow", [1, 2328])
    gm_d = din("gm", [1, 1024])
    ckw1_d = din("ckw1", [2048, 256])
    ckw2_d = din("ckw2", [256, 64])
    cvw1_d = din("cvw1", [2048, 256])
    cvw2_d = din("cvw2", [256, 64])
    ckpos_d = din("ckposT", [64, 32])
    cvpos_d = din("cvposT", [64, 32])
    wa_d = din("wa", [1024, 1024])
    wb_d = din("wb", [512, 1024])
    wo_d = din("wo", [1024, 1024])
    wg_d = din("wg", [1024, FF])
    wu_d = din("wu", [1024, FF])
    wd_d = din("wd", [FF, 1024])
    cf_d = din("constf", [128, NF])
    cb_d = din("constb", [128, NB], BF16)
    mmap_d = din("mmap", [128, NCT, 128], BF16)
    y_d = nc.dram_tensor("y", [S // 2, D], F32, kind="ExternalOutput").ap()
    pc_d = din("pc", [128, 260])
    if dbg:
        dya_d = nc.dram_tensor("dbg_ya", [S // 2, 1024], BF16, kind="ExternalOutput").ap()
        dyb_d = nc.dram_tensor("dbg_yb", [S // 2, 512], F32, kind="ExternalOutput").ap()

    def dscr(name, shape):
        return nc.dram_tensor(name, list(shape), BF16).ap()

    ckw1b_d = dscr("ckw1b", [2048, 256])
    ckw2b_d = dscr("ckw2b", [256, 64])
    cvw1b_d = dscr("cvw1b", [2048, 256])
    cvw2b_d = dscr("cvw2b", [256, 64])

    st = ExitStack()
    with st:
        P = Prog(nc, st)

        def MM(out, lhsT, rhs, start=True, stop=True, r=(), w=(), skip=False):
            n = int(np.prod(rhs.shape[1:]))
            P.op("pe", lambda e: e.matmul(out, lhsT=lhsT, rhs=rhs, start=start, stop=stop, skip_group_check=skip), r, w,
                 cost=(0.06 if n <= 65 else 0.12 + 0.00028 * n))

        def TR(out, in_, ident, r=(), w=()):
            P.op("pe", lambda e: e.transpose(out, in_, ident), r, w, cost=0.19)

        def ACT(out, in_, func, r=(), w=(), scale=1.0, bias=None, accum=None):
            def f(e):
                kw = {}
                if bias is not None:
                    kw["bias"] = bias
                if accum is not None:
                    kw["accum_out"] = accum
                return e.activation(out=out, in_=in_, func=func, scale=scale, **kw)
            P.op("act", f, r, w, cost=0.25 + 0.0006 * int(np.prod(out.shape[1:])))

        def TS(eng, out, in0, s1, s2, op0, op1=None, r=(), w=()):
            def f(e):
                if op1 is None:
                    return e.tensor_scalar(out=out, in0=in0, scalar1=s1, scalar2=None, op0=op0)
                return e.tensor_scalar(out=out, in0=in0, scalar1=s1, scalar2=s2, op0=op0, op1=op1)
            P.op(eng, f, r, w, cost=0.2 + 0.0007 * int(np.prod(out.shape[1:])))

        def TT(eng, out, in0, in1, op, r=(), w=()):
            P.op(eng, lambda e: e.tensor_tensor(out=out, in0=in0, in1=in1, op=op), r, w, cost=0.2 + 0.0009 * int(np.prod(out.shape[1:])))

        def STT(eng, out, in0, scalar, in1, op0, op1, r=(), w=()):
            P.op(eng, lambda e: e.scalar_tensor_tensor(out=out, in0=in0, scalar=scalar, in1=in1, op0=op0, op1=op1), r, w,
                 cost=0.2 + 0.0009 * int(np.prod(out.shape[1:])))

        def CP(eng, out, in_, r=(), w=()):
            if eng == "act":
                ACT(out, in_, AF.Copy, r, w)
            else:
                P.op(eng, lambda e: e.tensor_copy(out=out, in_=in_), r, w, cost=0.15 + 0.0005 * int(np.prod(out.shape[1:])))

        def MS(eng, ap, val, w=()):
            P.op(eng, lambda e: e.memset(ap, val), (), w)

        def DMA(eng, out, in_, r, w, sem):
            P.dma(eng, lambda e: e.dma_start(out=out, in_=in_), r, w, sem)

        def dram_tb(name):
            return TB(None, [P.buf(name)])

        cf = P.sb("cf", [128, NF], F32)
        cb = P.sb("cb", [128, NB], BF16)
        pvs = P.sb("pvs", [128, 128], F32)
        pvt = P.sb("pvt", [128, 128], F32)
        brow = P.sb("brow", [1, 2328], BF16)
        gmbc = P.sb("gmbc", [128, 1024], F32)
        DMA("sp", cf.t[:], cf_d, [], [cf], cf.bs[0])
        DMA("sp", cb.t[:], cb_d, [], [cb], cb.bs[0])
        DMA("sp", pvs.t[:], pv_d, [], [pvs], pvs.bs[0])
        DMA("pool", brow.t[:], brow_d, [], [brow], brow.bs[0])
        pcs = P.sb("pcs", [128, 260], F32)
        DMA("sp", pcs.t[:], pc_d, [], [pcs], pcs.bs[0])
        identf = cf.t[:, F_ID:F_ID + 128]
        identb = cb.t[:, B_ID:B_ID + 128]
        onesf = cf.t[:, F_ONES:F_ONES + 128]
        tri = cf.t[:, F_TRI:F_TRI + 128]

        banks = [P.ps("bank%d" % i, [128, 512], F32) for i in range(8)]
        mmp = Pool(banks[0:2])
        atp = Pool([banks[2], banks[3], banks[6], banks[7]])
        ocA, ocB = banks[4], banks[5]
        mlp = Pool([banks[0], banks[1]])
        bigp = Pool(banks[0:6])

        def bfv(bank):
            return bank.t[:, 0:512].bitcast(BF16)

        bk = mmp.next()
        TR(bk.t[:, 0:128], pvs.t[:], identf, [pvs, cf], [bk])
        CP("dve", pvt.t[:], bk.t[:, 0:128], [bk], [pvt])
        def pcol(c):
            return pvt.t[:, c:c + 1]

        negbf = P.sb("negbf", [4, 1], F32)
        TT("dve", negbf.t[:], pvt.t[0:4, R_BF:R_BF + 1], pvt.t[0:4, R_FB:R_FB + 1], ALU.add, [pvt], [negbf])
        TS("dve", negbf.t[:], negbf.t[:], -1.0, None, ALU.mult, None, [negbf], [negbf])

        kslcT = P.sb("kslcT", [128, S], BF16)
        kslc_b = [P.buf("kslc") for _ in range(NT)]
        vslc = P.sb("vslc", [128, NT, 2, 65], BF16)
        vslc_b = [P.buf("vslc") for _ in range(NT)]
        kwinT = P.sb("kwinT", [128, 8 * 128], BF16)
        kwin_b = [P.buf("kwin") for _ in range(8)]
        vwin = P.sb("vwin", [128, 8, 2, 65], BF16)
        vwin_b = [P.buf("vwin") for _ in range(8)]
        kcT = P.sb("kcT", [128, CPAD], BF16)
        vcx = P.sb("vcx", [128, NCT, 2, 193], BF16)
        stateF = P.sb("stateF", [128, 4, 257], F32)
        state_b = [P.buf("st") for _ in range(4)]
        Cbf = P.sb("Cbf", [128, 4, 258], BF16)
        cbf_b = [P.buf("cbf") for _ in range(4)]
        halo = P.sb("halo", [128, 8, 3], F32)
        MS("pool", vslc.t[:], 1.0, [vslc_b])
        MS("pool", vwin.t[:], 1.0, [vwin_b])
        MS("pool", kcT.t[:], 0.0, [kcT])
        MS("pool", vcx.t[:], 1.0, [vcx])
        MS("pool", stateF.t[:], 0.0, [state_b])
        MS("pool", Cbf.t[:], 0.0, [cbf_b])
        MS("pool", halo.t[:], 0.0, [halo])
        for gi in range(2):
            DMA("sp", vcx.t[:, :, gi, 65:193], mmap_d, [], [vcx], vcx.bs[0])

        def small(name, shape, dt=F32):
            return P.sb(name, shape, dt)

        scT = small("scT", [128, TPG, 12])
        ssx = small("ssx", [128, 4])
        decbc = small("decbc", [128, TPG * 4])
        dec4 = small("dec4", [4, TPG])
        rdg = small("rdg", [4, TPG * 4])
        t1s = [small("t1s%d" % i, [128, 4]) for i in range(2)]
        ssq = [small("ssq%d" % i, [128, 1]) for i in range(2)]
        rs4 = small("rs4", [128, 4])
        w4 = small("w4", [128, 4])
        m8 = small("m8", [128, 16])
        gbrs = [small("gbr%d" % i, [128, TPG, 24]) for i in range(2)]
        negselT = [small("negselT%d" % i, [128, 256], BF16) for i in range(2)]
        selbb = small("selbb", [128, 128], BF16)
        ktok = [small("ktok%d" % i, [128, 128], BF16) for i in range(2)]
        stm = [small("stm%d" % i, [128, 128], BF16) for i in range(2)]
        junk = small("junk", [128, 256], BF16)
        cconst = small("cconst", [128, 2])
        posT = small("posT", [128, 32], F32)
        posTb = small("posTb", [128, 32], BF16)
        w2s = small("w2s", [128, 2, 64], BF16)

        OB = 4
        NTL = 128 * OB
        AR_BYTES = 126 * 1024
        ar = Arena(P, "arena", AR_BYTES)
        NSLOT = 5
        SLOTB = 8192
        ring = [ar.view(i * SLOTB, [128, SLOTB // 2], BF16) for i in range(NSLOT)]
        ringi = [0]
        al = Bump(ar, NSLOT * SLOTB)
        xl = [al.take([128, 1024], F32) for _ in range(TPG)]
        hnT = al.take([128, 8, GT], BF16)
        assert len(hnT.bs) == 8
        hnc = [TB(None, [hnT.bs[i]]) for i in range(8)]
        sqj = al.take([128, 1024], BF16)
        xTt = [al.take([128, NTL], F32) for _ in range(8)]
        yaT_t = al.take([128, 8, NTL], BF16)
        ybT_t = al.take([128, 4, NTL], BF16)
        base_phase = al.off
        a1 = Bump(ar, 0)
        kcmpT = a1.take([128, 16, S // 16], BF16)
        vcmpT = a1.take([128, 16, S // 16], BF16)
        a1 = Bump(ar, max(base_phase, a1.off))
        wcmp = a1.take([128, 8, 512], BF16)
        w1s = a1.take([128, 32, 256], BF16)
        hT = a1.take([128, 2, CPAD], BF16)
        gmrow = ar.view(base_phase, [128, 1024], F32)
        a2 = Bump(ar, base_phase)
        vt = a2.take([128, TPG, 4, 258], BF16)
        sigo = a2.take([128, 1024], F32)
        nqT = a2.take([128, 4, 128], BF16)
        qkT = a2.take([128, 8, GT], BF16)
        pre = [a2.take([128, GT + 3], F32) for _ in range(2)]
        cvt = [a2.take([128, GT], F32) for _ in range(2)]
        ya = a2.take([128, 1024], BF16)
        yb = a2.take([128, 512], F32)
        gA = a2.take([128, GT], F32)
        gL = a2.take([128, GT], F32)
        gB = a2.take([128, GT], F32)
        gG = a2.take([128, GT], F32)
        gN = a2.take([128, GT], F32)
        gek = a2.take([128, GT], F32)
        geq = a2.take([128, GT], F32)
        gem = a2.take([128, GT], F32)
        dtmp = [a2.take([128, 257], F32) for _ in range(2)]
        hb = [a2.take([128, 256], F32) for _ in range(2)]
        ptb = [a2.take([128, 512], BF16) for _ in range(4)]
        impb = a2.take([128, 128], F32)
        scb = a2.take([128, 128], F32)
        sc2b = a2.take([128, 128], F32)
        selb = a2.take([128, 128], F32)
        ybtmp = a2.take([128, 256], F32)
        mix_end = a2.off
        a3 = Bump(ar, base_phase)
        hnTt = a3.take([128, 8, NTL], BF16)
        mergedT = a3.take([128, 8, NTL], BF16)
        actb = [a3.take([128, 4, NTL], BF16) for _ in range(2)]
        tmpm = [a3.take([128, NTL], F32) for _ in range(2)]
        rsbT = a3.take([128, NTL], F32)
        sqtT = [a3.take([128, NTL], F32) for _ in range(2)]
        ot = a3.take([128, 1024], F32)
        print("arena use: phase1 %d  mixer %d  tail %d  of %d" % (a1.off, mix_end, a3.off, AR_BYTES))
        assert max(a1.off, a2.off, a3.off) <= AR_BYTES
        ptp = Pool(ptb)

        DMA("sp", gmrow.t[0:1, :], gm_d, [], [gmrow], gmrow.bs[0])
        for hf in range(2):
            bk = mmp.next()
            MM(bk.t[:, 0:512], cf.t[0:1, F_ONES:F_ONES + 128], gmrow.t[0:1, hf * 512:(hf + 1) * 512], True, True, [cf, gmrow], [bk])
            CP("dve", gmbc.t[:, hf * 512:(hf + 1) * 512], bk.t[:, 0:512], [bk], [gmbc])

        TS("dve", gmbc.t[:], gmbc.t[:], 0.5, None, ALU.mult, None, [gmbc], [gmbc])
        Bcar = small("Bcar", [4, 1])
        Gcar = small("Gcar", [4, 1])
        NGcar = small("NGcar", [4, 1])
        MS("pool", Bcar.t[:], 0.0, [Bcar])
        MS("pool", Gcar.t[:], 0.0, [Gcar])
        MS("pool", NGcar.t[:], 0.0, [NGcar])

        def wtile(blk):
            src_ap, a, b, src_tb = blk
            slot = ring[ringi[0] % NSLOT]
            ringi[0] += 1
            assert a * b * 2 <= SLOTB
            v2 = slot.t[:, 0:a * b]
            DMA("sp", v2, src_ap, [src_tb], [slot], slot.bs[0])
            return TB(v2.rearrange("p (a b) -> p a b", a=a), slot.bs)

        def load_x(g):
            for ti in range(TPG):
                r0 = (g * TPG + ti) * 128
                DMA("sp", xl[ti].t, x_d[r0:r0 + 128, :], [], [xl[ti]], xl[ti].bs[0])

        def prep_x(g, nxt, own_cj, p1=False):
            if own_cj is not None:
                for hf in range(2):
                    bk = mmp.next()
                    for c in range(4):
                        TR(bk.t[:, c * 128:(c + 1) * 128], xl[1].t[:, (hf * 4 + c) * 128:(hf * 4 + c + 1) * 128], identf, [xl[1], cf], [bk])
                    for c in range(4):
                        dc = hf * 4 + c
                        CP("act" if hf else "dve", xTt[dc].t[:, own_cj:own_cj + 128], bk.t[:, c * 128:(c + 1) * 128], [bk], [xTt[dc]])
            for ti in range(TPG):
                ACT(sqj.t, xl[ti].t, AF.Square, [xl[ti]], [sqj, ssx], accum=ssx.t[:, ti:ti + 1])
            ACT(ssx.t[:, 2:4], ssx.t[:, 0:2], AF.Ln, [ssx], [ssx], scale=1.0 / 1024, bias=1e-6)
            ACT(ssx.t[:, 2:4], ssx.t[:, 2:4], AF.Exp, [ssx], [ssx], scale=-0.5)
            for ti in range(TPG):
                if p1:
                    TS("dve", xl[ti].t, xl[ti].t, ssx.t[:, 2 + ti:3 + ti], None, ALU.mult, None, [xl[ti], ssx], [xl[ti]])
                else:
                    ACT(xl[ti].t, xl[ti].t, AF.Copy, [xl[ti], ssx], [xl[ti]], scale=ssx.t[:, 2 + ti:3 + ti])
            for dc in range(8):
                bk = mmp.next()
                for ti in range(TPG):
                    TR(bk.t[:, ti * 128:(ti + 1) * 128], xl[ti].t[:, dc * 128:(dc + 1) * 128], identf, [xl[ti], cf], [bk])
                if p1 and dc % 2 == 0:
                    TS("dve", hnT.t[:, dc, :], bk.t[:, 0:GT], pcol(R_G1 + dc), None, ALU.mult, None, [bk, pvt], [hnc[dc]])
                else:
                    ACT(hnT.t[:, dc, :], bk.t[:, 0:GT], AF.Copy, [bk, pvt], [hnc[dc]], scale=pcol(R_G1 + dc))
            if nxt is not None:
                load_x(nxt)

        def cast(dst, src, rows, step, name):
            tb = dram_tb(name)
            for r0 in range(0, rows, step):
                r1 = min(rows, r0 + step)
                DMA("pool", dst[r0:r1, :], src[r0:r1, :], [], [tb], tb.bs[0])
            return tb

        def cast_tiled(src_d, kcn, widths, name, first=()):
            nmax = max(widths)
            scr = nc.dram_tensor(name, [len(widths), 128, kcn * nmax], BF16).ap()
            offs = [sum(widths[:b]) for b in range(len(widths))]
            blocks = [None] * len(widths)
            order = list(first) + [b for b in range(len(widths)) if b not in first]
            for b in order:
                n = widths[b]
                c = offs[b]
                tb = dram_tb("%s_%d" % (name, b))
                dst = scr[b, :, 0:kcn * n].rearrange("p (kc n) -> p kc n", kc=kcn)
                srcv = src_d[:, c:c + n].rearrange("(kc p) n -> p kc n", p=128)
                DMA("pool", dst, srcv, [], [tb], tb.bs[0])
                blocks[b] = (scr[b, :, 0:kcn * n], kcn, n, tb)
            return blocks

        winT = cast_tiled(win_d, 8, [512] * 12 + [288], "winT", first=(7,))
        waT = cast_tiled(wa_d, 8, [512, 512], "waT")
        wbT = cast_tiled(wb_d, 4, [1024], "wbT")
        woT = cast_tiled(wo_d, 8, [512, 512], "woT")
        wgT = cast_tiled(wg_d, 8, [512] * 5 + [256], "wgT")
        wuT = cast_tiled(wu_d, 8, [512] * 5 + [256], "wuT")
        def cast_rows(src_d, blocks, ncols, name):
            scr = nc.dram_tensor(name, [len(blocks), 128, max(blocks) * ncols], BF16).ap()
            tb = dram_tb(name)
            r = 0
            out = []
            for b, nh in enumerate(blocks):
                dst = scr[b, :, 0:nh * ncols].rearrange("p (hc n) -> p hc n", hc=nh)
                srcv = src_d[r:r + nh * 128, :].rearrange("(hc p) n -> p hc n", p=128)
                DMA("pool", dst, srcv, [], [tb], tb.bs[0])
                out.append((scr[b, :, 0:nh * ncols], nh, ncols, tb))
                r += nh * 128
            return out

        wd2T = cast_rows(wd_d, [4, 4, 4, 4, 4, 2], 1024, "wd2T")
        def cast_w1(dst, src, name):
            tb = dram_tb(name)
            DMA("pool", dst.rearrange("(d l) h -> d l h", l=32), src.rearrange("(l d) h -> d l h", d=64), [], [tb], tb.bs[0])
            return tb

        ckw1b = cast_w1(ckw1b_d, ckw1_d, "ckw1b")
        ckw2b = cast(ckw2b_d, ckw2_d, 256, 256, "ckw2b")
        cvw1b = cast_w1(cvw1b_d, cvw1_d, "cvw1b")
        cvw2b = cast(cvw2b_d, cvw2_d, 256, 256, "cvw2b")

        mmp.items = list(banks[0:6])
        DMA("sp", wcmp.t, winT[7][0].rearrange("p (kc n) -> p kc n", kc=8), [winT[7][3]], [wcmp], wcmp.bs[0])
        load_x(0)
        for g in range(NG):
            prep_x(g, (g + 1) % NG, None, p1=True)
            for oc in range(2):
                bk = bigp.next()
                for kc in range(8):
                    MM(bk.t[:, 0:GT], wcmp.t[:, kc, 256 + oc * 128:256 + (oc + 1) * 128], hnT.t[:, kc, :], kc == 0, kc == 7, [wcmp, hnc[kc]], [bk])
                dst = kcmpT if oc == 0 else vcmpT
                q0 = g * (GT // 16)
                dsti = dst.t[:, :, q0:q0 + GT // 16]
                srci = bk.t[:, 0:GT].rearrange("p (q r) -> p r q", r=16)
                if oc == 0:
                    ACT(dsti, srci, AF.Identity, [bk, pvt], [dst], bias=pcol(R_BCM + 30 + oc))
                else:
                    TS("dve", dsti, srci, pcol(R_BCM + 30 + oc), None, ALU.add, None, [bk, pvt], [dst])

        MS("pool", hT.t, 0.0, [hT])
        for kv in range(2):
            w1b_d, w1tb = (ckw1b_d, ckw1b) if kv == 0 else (cvw1b_d, cvw1b)
            w2b_d, w2tb = (ckw2b_d, ckw2b) if kv == 0 else (cvw2b_d, cvw2b)
            pos_d = ckpos_d if kv == 0 else cvpos_d
            src = kcmpT if kv == 0 else vcmpT
            for half in range(2):
                DMA("sp", w1s.t[half * 64:(half + 1) * 64, :, :], w1b_d.rearrange("(d l) h -> d l h", l=32), [w1tb], [w1s], w1s.bs[0])
                DMA("sp", posT.t[half * 64:(half + 1) * 64, :], pos_d, [], [posT], posT.bs[0])
            DMA("sp", w2s.t[:], w2b_d.rearrange("(hc p) d -> p hc d", p=128), [w2tb], [w2s], w2s.bs[0])
            CP("dve", posTb.t[:], posT.t[:], [posT], [posTb])
            for hc in range(2):
                bk = mmp.next()
                for l in range(32):
                    MM(bk.t[:, 0:1], w1s.t[0:64, l, hc * 128:(hc + 1) * 128], posTb.t[0:64, l:l + 1], l == 0, l == 31, [w1s, posTb], [bk])
                CP("dve", cconst.t[:, hc:hc + 1], bk.t[:, 0:1], [bk], [cconst])
            for gi in range(2):
                pb = 64 * gi
                for hc in range(2):
                    for c0 in range(0, NCMP, 512):
                        n = min(512, NCMP - c0)
                        bk = mmp.next()
                        for l in range(32):
                            a_, r_ = divmod(l, 16)
                            MM(bk.t[:, 0:n], w1s.t[pb:pb + 64, l, hc * 128:(hc + 1) * 128],
                               src.t[pb:pb + 64, r_, c0 + a_:c0 + a_ + n], l == 0, l == 31, [w1s, src], [bk])
                        ACT(hT.t[:, hc, c0:c0 + n], bk.t[:, 0:n], AF.Silu, [bk, cconst], [hT], bias=cconst.t[:, hc:hc + 1])
                if kv == 0:
                    for c0 in range(0, NCMP, 512):
                        n = min(512, NCMP - c0)
                        bk = mmp.next()
                        for hc in range(2):
                            MM(bk.t[pb:pb + 64, 0:n], w2s.t[:, hc, :], hT.t[:, hc, c0:c0 + n], hc == 0, hc == 1, [w2s, hT], [bk])
                        CP("dve", kcT.t[pb:pb + 64, c0:c0 + n], bk.t[pb:pb + 64, 0:n], [bk], [kcT])
                else:
                    for ct in range(NCT):
                        bk = mmp.next()
                        for hc in range(2):
                            MM(bk.t[:, 0:64], hT.t[:, hc, ct * 128:(ct + 1) * 128], w2s.t[:, hc, :], hc == 0, hc == 1, [w2s, hT], [bk])
                        CP("dve", vcx.t[:, ct, gi, 0:64], bk.t[:, 0:64], [bk], [vcx])

        def bias_mm(bk, n, boff):
            MM(bk.t[:, 0:n], cb.t[0:1, B_ONES:B_ONES + 128], brow.t[0:1, boff:boff + n], False, True, [cb, brow], [bk])

        def ytr_a(g):
            cj = (g % OB) * 128
            bk = mmp.next()
            for c in range(8):
                TR(bfv(bk)[:, c * 128:(c + 1) * 128], ya.t[:, c * 128:(c + 1) * 128], identb, [ya, cb], [bk])
            CP("dve", yaT_t.t[:, :, cj:cj + 128], bfv(bk)[:, 0:1024].rearrange("p (a b) -> p a b", a=8), [bk], [yaT_t])

        def ytr_b(g):
            cj = (g % OB) * 128
            bk = mmp.next()
            for c in range(4):
                TR(bk.t[:, c * 128:(c + 1) * 128], yb.t[:, c * 128:(c + 1) * 128], identf, [yb, cf], [bk])
            CP("dve", ybT_t.t[:, :, cj:cj + 128], bk.t[:, 0:512].rearrange("p (a b) -> p a b", a=4), [bk], [ybT_t])

        def dense_tail(g0):
            N = NTL

            def normT(gcol, out_fn):
                bk = bigp.next()
                for dc in range(8):
                    sq = sqtT[dc % 2]
                    ACT(sq.t, xTt[dc].t, AF.Square, [xTt[dc]], [sq])
                    MM(bk.t[:, 0:N], onesf, sq.t, dc == 0, dc == 7, [sq, cf], [bk])
                ACT(rsbT.t, bk.t[:, 0:N], AF.Ln, [bk], [rsbT], scale=1.0 / 1024, bias=1e-6)
                ACT(rsbT.t, rsbT.t, AF.Exp, [rsbT], [rsbT], scale=-0.5)
                for dc in range(8):
                    o = out_fn(dc)
                    STT("dve", o[0], xTt[dc].t, pcol(gcol + dc), rsbT.t, ALU.mult, ALU.mult, [xTt[dc], rsbT, pvt], [o[1]])

            def proj(wblk, oc, rhsT, nk):
                bk = bigp.next()
                for kc in range(nk):
                    MM(bk.t[:, 0:N], wblk.t[:, kc, oc * 128:(oc + 1) * 128], rhsT.t[:, kc, :], kc == 0, kc == nk - 1, [wblk, rhsT], [bk])
                return bk

            normT(R_G1, lambda dc: (hnTt.t[:, dc, :], hnTt))
            for blk in range(2):
                wga = wtile(winT[3 + blk])
                wak = wtile(waT[blk])
                wgb = wtile(winT[5 + blk])
                wbk = wtile(wbT[0])
                for oc in range(4):
                    dc = blk * 4 + oc
                    b = proj(wga, oc, hnTt, 8)
                    ACT(tmpm[0].t, b.t[:, 0:N], AF.Sigmoid, [b, pvt], [tmpm[0]], bias=pcol(R_BCM + 12 + dc))
                    b = proj(wak, oc, yaT_t, 8)
                    TT("dve", tmpm[0].t, tmpm[0].t, b.t[:, 0:N], ALU.mult, [tmpm[0], b], [tmpm[0]])
                    b = proj(wgb, oc, hnTt, 8)
                    ACT(tmpm[1].t, b.t[:, 0:N], AF.Sigmoid, [b, pvt], [tmpm[1]], bias=pcol(R_BCM + 20 + dc))
                    b = bigp.next()
                    for kc in range(4):
                        MM(b.t[:, 0:N], wbk.t[:, kc, dc * 128:(dc + 1) * 128], ybT_t.t[:, kc, :], kc == 0, kc == 3, [wbk, ybT_t], [b])
                    TT("dve", tmpm[1].t, tmpm[1].t, b.t[:, 0:N], ALU.mult, [tmpm[1], b], [tmpm[1]])
                    TT("pool", mergedT.t[:, dc, :], tmpm[0].t, tmpm[1].t, ALU.add, [tmpm[0], tmpm[1]], [mergedT])
            for blk in range(2):
                wok = wtile(woT[blk])
                for oc in range(4):
                    dc = blk * 4 + oc
                    b = proj(wok, oc, mergedT, 8)
                    TT("dve", xTt[dc].t, xTt[dc].t, b.t[:, 0:N], ALU.add, [xTt[dc], b], [xTt[dc]])
            normT(R_G2, lambda dc: (hnTt.t[:, dc, :], hnTt))
            for hb_ in range(6):
                nh = 4 if hb_ < 5 else 2
                wgk = wtile(wgT[hb_])
                wuk = wtile(wuT[hb_])
                wdk = wtile(wd2T[hb_])
                ab = actb[hb_ % 2]
                for oc in range(nh):
                    bg = proj(wgk, oc, hnTt, 8)
                    bu = proj(wuk, oc, hnTt, 8)
                    ACT(tmpm[oc % 2].t, bg.t[:, 0:N], AF.Silu, [bg], [tmpm[oc % 2]])
                    TT("dve", ab.t[:, oc, :], tmpm[oc % 2].t, bu.t[:, 0:N], ALU.mult, [tmpm[oc % 2], bu], [ab])
                for dc in range(8):
                    b = bigp.next()
                    for oc in range(nh):
                        MM(b.t[:, 0:N], wdk.t[:, oc, dc * 128:(dc + 1) * 128], ab.t[:, oc, :], oc == 0, oc == nh - 1, [wdk, ab], [b])
                    TT("dve", xTt[dc].t, xTt[dc].t, b.t[:, 0:N], ALU.add, [xTt[dc], b], [xTt[dc]])
            normT(R_GF, lambda dc: (xTt[dc].t, xTt[dc]))
            for j in range(OB):
                for hf in range(2):
                    b = bigp.next()
                    for c in range(4):
                        dc = hf * 4 + c
                        TR(b.t[:, c * 128:(c + 1) * 128], xTt[dc].t[:, j * 128:(j + 1) * 128], identf, [xTt[dc], cf], [b])
                    CP("act", ot.t[:, hf * 512:(hf + 1) * 512], b.t[:, 0:512], [b], [ot])
                DMA("pool", y_d[(g0 + j) * 128:(g0 + j + 1) * 128, :], ot.t, [ot], [], ot.bs[0])

        def nq_part(g):
            wn = wtile(winT[2])
            for oc in range(4):
                bk = mmp.next()
                for kc in range(8):
                    MM(bk.t[:, 0:128], wn.t[:, kc, oc * 128:(oc + 1) * 128], hnT.t[:, kc, 128:256], kc == 0, kc == 7, [wn, hnc[kc]], [bk])
                ACT(nqT.t[:, oc, :], bk.t[:, 0:128], AF.Identity, [bk, pvt], [nqT], bias=pcol(R_BCM + 8 + oc))

        def B_part(g):
            tiles = [g * TPG + ti for ti in range(TPG)]
            gbr = gbrs[g % 2]
            prep_x(g, g + 1 if g + 1 < NG else None, (g % OB) * 128)

            wsm = wtile(winT[12])
            bi = mlp.next()
            for kc in range(8):
                MM(bi.t[0:4, 0:GT], wsm.t[:, kc, 280:284], hnT.t[:, kc, :], kc == 0, kc == 7, [wsm, hnc[kc]], [bi])
            bff = mlp.next()
            for kc in range(8):
                MM(bff.t[0:4, 0:GT], wsm.t[:, kc, 284:288], hnT.t[:, kc, :], kc == 0, kc == 7, [wsm, hnc[kc]], [bff])
            ACT(gA.t[0:4, 0:GT], bi.t[0:4, 0:GT], AF.Identity, [bi, pvt], [gA], bias=pvt.t[0:4, R_BI:R_BI + 1])
            ACT(gL.t[0:4, 0:GT], bff.t[0:4, 0:GT], AF.Exp, [bff, negbf], [gL], scale=-1.0, bias=negbf.t[0:4, 0:1])
            ACT(gL.t[0:4, 0:GT], gL.t[0:4, 0:GT], AF.Ln, [gL], [gL], bias=1.0)
            TS("dve", gL.t[0:4, 0:GT], gL.t[0:4, 0:GT], -1.0, None, ALU.mult, None, [gL], [gL])
            if g == 0:
                TS("dve", gL.t[0:4, 0:128], gL.t[0:4, 0:128], pcs.t[0:4, 0:1], None, ALU.mult, None, [gL, pcs], [gL])
                TS("dve", gA.t[0:4, 0:128], gA.t[0:4, 0:128], pcs.t[0:4, 0:1], pcs.t[0:4, 1:2], ALU.mult, ALU.add, [gA, pcs], [gA])
            P.op("dve", lambda e: e.tensor_tensor_scan(out=gB.t[0:4, 0:GT], data0=cf.t[0:4, F_ONES:F_ONES + GT], data1=gL.t[0:4, 0:GT],
                                                       initial=Bcar.t[0:4, 0:1], op0=ALU.mult, op1=ALU.add), [cf, gL, Bcar], [gB])
            TT("dve", gA.t[0:4, 0:GT], gA.t[0:4, 0:GT], gB.t[0:4, 0:GT], ALU.subtract, [gA, gB], [gA])
            P.op("dve", lambda e: e.tensor_tensor_scan(out=gG.t[0:4, 0:GT], data0=cf.t[0:4, F_ONES:F_ONES + GT], data1=gA.t[0:4, 0:GT],
                                                       initial=Gcar.t[0:4, 0:1], op0=ALU.mult, op1=ALU.max), [cf, gA, Gcar], [gG])
            TS("dve", gN.t[0:4, 0:GT], gG.t[0:4, 0:GT], -1.0, None, ALU.mult, None, [gG], [gN])
            for ti in range(TPG):
                c0 = ti * 128
                mu = Gcar.t[0:4, 0:1] if ti == 0 else gG.t[0:4, c0 - 1:c0]
                nmu = NGcar.t[0:4, 0:1] if ti == 0 else gN.t[0:4, c0 - 1:c0]
                ACT(gek.t[0:4, c0:c0 + 128], gA.t[0:4, c0:c0 + 128], AF.Exp, [gA, gN, NGcar], [gek], bias=nmu)
                ACT(geq.t[0:4, c0:c0 + 128], gG.t[0:4, c0:c0 + 128], AF.Exp, [gG, Gcar], [geq], scale=-1.0, bias=mu)
            TT("dve", gL.t[0:4, 0:GT], gB.t[0:4, 0:GT], gG.t[0:4, 0:GT], ALU.add, [gB, gG], [gL])
            ACT(gem.t[0:4, 0:GT], gL.t[0:4, 0:GT], AF.Exp, [gL], [gem], scale=-1.0)
            CP("dve", Bcar.t[0:4, 0:1], gB.t[0:4, GT - 1:GT], [gB], [Bcar])
            CP("dve", Gcar.t[0:4, 0:1], gG.t[0:4, GT - 1:GT], [gG], [Gcar])
            CP("dve", NGcar.t[0:4, 0:1], gN.t[0:4, GT - 1:GT], [gN], [NGcar])
            for ti in range(TPG):
                c0 = ti * 128
                bk = mlp.next()
                TR(bk.t[:, 0:4], gek.t[0:4, c0:c0 + 128], cf.t[0:4, 0:4], [gek, cf], [bk])
                TR(bk.t[:, 4:8], geq.t[0:4, c0:c0 + 128], cf.t[0:4, 0:4], [geq, cf], [bk])
                TR(bk.t[:, 8:12], gem.t[0:4, c0:c0 + 128], cf.t[0:4, 0:4], [gem, cf], [bk])
                CP("dve", scT.t[:, ti, :], bk.t[:, 0:12], [bk], [scT])
                TS("dve", scT.t[:, ti, 0:4], scT.t[:, ti, 0:4], float(0.25 * 128 ** -0.5), None, ALU.mult, None, [scT], [scT])
            CP("dve", dec4.t[:, :], geq.t[0:4, 127:GT:128], [geq], [dec4])
            TT("dve", rdg.t[:, :].rearrange("p (a b) -> p a b", a=TPG), dec4.t[:, :].unsqueeze(2).to_broadcast([4, TPG, 4]),
               cf.t[0:4, 0:4].unsqueeze(1).to_broadcast([4, TPG, 4]), ALU.mult, [dec4, cf], [rdg])
            bk = mlp.next()
            MM(bk.t[:, 0:TPG * 4], cf.t[0:4, F_ONES:F_ONES + 128], rdg.t[:, :], True, True, [cf, rdg], [bk])
            CP("dve", decbc.t[:, :], bk.t[:, 0:TPG * 4], [bk], [decbc])

            for ti in range(TPG):
                jq = tiles[ti]
                slot = jq % 8
                for which in range(3):
                    if which == 2 and ti == 0:
                        continue
                    c0, n, boff = [(0, 128, 2048), (128, 128, 2176), (256, 24, 2304)][which]
                    bk = mmp.next()
                    for kc in range(8):
                        MM(bk.t[:, 0:n], hnT.t[:, kc, ti * 128:(ti + 1) * 128], wsm.t[:, kc, c0:c0 + n], kc == 0, False, [wsm, hnc[kc]], [bk])
                    bias_mm(bk, n, boff)
                    if which == 0:
                        CP("dve", vslc.t[:, jq, :, 0:64], bk.t[:, 0:128].rearrange("p (a b) -> p a b", a=2), [bk], [vslc_b[jq]])
                    elif which == 1:
                        CP("dve", vwin.t[:, slot, :, 0:64], bk.t[:, 0:128].rearrange("p (a b) -> p a b", a=2), [bk], [vwin_b[slot]])
                    else:
                        ACT(gbr.t[:, ti, :], bk.t[:, 0:24], AF.Tanh, [bk], [gbr], scale=0.5)
                        TS("dve", gbr.t[:, ti, :], gbr.t[:, ti, :], 1.0, 0.5, ALU.add, ALU.mult, [gbr], [gbr])

            for hf in range(2):
                wv = wtile(winT[8 + hf])
                for ti in range(TPG):
                    bk = mmp.next()
                    for kc in range(8):
                        MM(bk.t[:, 0:512], hnT.t[:, kc, ti * 128:(ti + 1) * 128], wv.t[:, kc, :], kc == 0, False, [wv, hnc[kc]], [bk])
                    bias_mm(bk, 512, hf * 512)
                    for hh in range(2):
                        h = hf * 2 + hh
                        ACT(vt.t[:, ti, h, 0:256], bk.t[:, hh * 256:(hh + 1) * 256], AF.Copy, [bk, scT], [vt], scale=scT.t[:, ti, h:h + 1])
            for ti in range(TPG):
                CP("dve", vt.t[:, ti, :, 256], scT.t[:, ti, 0:4], [scT], [vt])
            for hf in range(2):
                wo_ = wtile(winT[10 + hf])
                for ti in (1,):
                    bk = mmp.next()
                    for kc in range(8):
                        MM(bk.t[:, 0:512], hnT.t[:, kc, ti * 128:(ti + 1) * 128], wo_.t[:, kc, :], kc == 0, False, [wo_, hnc[kc]], [bk])
                    bias_mm(bk, 512, 1024 + hf * 512)
                    ACT(sigo.t[:, hf * 512:(hf + 1) * 512], bk.t[:, 0:512], AF.Tanh, [bk], [sigo], scale=0.5)
                    STT("dve", sigo.t[:, hf * 512:(hf + 1) * 512], sigo.t[:, hf * 512:(hf + 1) * 512], 1.0, gmbc.t[:, hf * 512:(hf + 1) * 512], ALU.add, ALU.mult, [sigo, gmbc], [sigo])

            def cm_chunk(wblk, oc, lo=0, hi=GT):
                bk = mmp.next()
                for kc in range(8):
                    MM(bk.t[:, 0:hi - lo], wblk.t[:, kc, oc * 128:(oc + 1) * 128], hnT.t[:, kc, lo:hi], kc == 0, kc == 7, [wblk, hnc[kc]], [bk])
                return bk

            wqs = {}

            def conv_s1(c):
                blk, oc = divmod(c, 4)
                if blk not in wqs:
                    wqs[blk] = wtile(winT[blk])
                bk = cm_chunk(wqs[blk], oc)
                pr = pre[c % 2]
                CP("pool", pr.t[:, 0:3], halo.t[:, c, :], [halo], [pr])
                ACT(pr.t[:, 3:3 + GT], bk.t[:, 0:GT], AF.Identity, [bk, pvt], [pr], bias=pcol(R_BCM + c))
                CP("pool", halo.t[:, c, :], pr.t[:, GT:GT + 3], [pr], [halo])

            def conv_s2(c):
                pr = pre[c % 2]
                cv = cvt[c % 2]
                if g == 0:
                    TS("dve", pr.t[:, 3:131], pr.t[:, 3:131], pcs.t[:, 0:1], None, ALU.mult, None, [pr, pcs], [pr])
                TS("dve", cv.t, pr.t[:, 0:GT], pcol(R_CW + c), pcol(R_CB + c), ALU.mult, ALU.add, [pr, pvt], [cv])
                for j in range(1, 4):
                    STT("dve", cv.t, pr.t[:, j:j + GT], pcol(R_CW + j * 8 + c), cv.t, ALU.mult, ALU.add, [pr, pvt, cv], [cv])
                ACT(pr.t[:, 0:GT], cv.t, AF.Tanh, [cv], [pr], scale=0.5)
                STT("dve", qkT.t[:, c, :], pr.t[:, 0:GT], 1.0, cv.t, ALU.add, ALU.mult, [pr, cv], [qkT])

            conv_s1(0)
            for c in range(8):
                if c + 1 < 8:
                    conv_s1(c + 1)
                conv_s2(c)
            wk = wtile(winT[7])
            bk = cm_chunk(wk, 0)
            ACT(kslcT.t[:, g * GT:(g + 1) * GT], bk.t[:, 0:GT], AF.Identity, [bk, pvt], [kslc_b[j] for j in tiles], bias=pcol(R_BCM + 28))
            bk = cm_chunk(wk, 1)
            s0 = tiles[0] % 8
            ACT(kwinT.t[:, s0 * 128:(s0 + TPG) * 128], bk.t[:, 0:GT], AF.Identity, [bk, pvt], [kwin_b[j % 8] for j in tiles], bias=pcol(R_BCM + 29))

            for ti in range(TPG):
                c0 = ti * 128
                for h in range(4):
                    par = (ti * 4 + h) % 2
                    kT_ap = qkT.t[:, 4 + h, c0:c0 + 128]
                    qT_ap = qkT.t[:, h, c0:c0 + 128]
                    b1 = mlp.next()
                    TR(bfv(b1)[:, 0:128], kT_ap, identb, [qkT, cb], [b1])
                    CP("act", ktok[par].t[:], bfv(b1)[:, 0:128], [b1], [ktok[par]])
                    if ti == 1:
                      b2 = mlp.next()
                      MM(b2.t[:, 0:128], kT_ap, qT_ap, True, True, [qkT], [b2])
                      TT("dve", stm[par].t[:], b2.t[:, 0:128], tri, ALU.mult, [b2, cf], [stm[par]])
                      b3 = mlp.next()
                      MM(b3.t[:, 0:257], stm[par].t[:], vt.t[:, ti, h, 0:257], True, False, [stm[par], vt], [b3])
                      MM(b3.t[:, 0:257], qT_ap, Cbf.t[:, h, 0:257], False, True, [qkT, cbf_b[h]], [b3])
                      t1 = t1s[par]
                      eq_h = scT.t[:, ti, 4 + h:5 + h]
                      em_h = scT.t[:, ti, 8 + h:9 + h]
                      ACT(t1.t[:, 0:1], b3.t[:, 256:257], AF.Abs, [b3, scT], [t1], scale=eq_h)
                      TS("dve", t1.t[:, 0:1], t1.t[:, 0:1], em_h, None, ALU.max, None, [t1, scT], [t1])
                      P.op("dve", lambda e, t1=t1: e.reciprocal(out=t1.t[:, 3:4], in_=t1.t[:, 0:1]), [t1], [t1])
                      TS("dve", t1.t[:, 1:2], t1.t[:, 3:4], eq_h, None, ALU.mult, None, [t1, scT], [t1])
                      ACT(hb[par].t, b3.t[:, 0:256], AF.Copy, [b3, t1], [hb[par]], scale=t1.t[:, 1:2])
                      ACT(junk.t[:, :], hb[par].t, AF.Square, [hb[par]], [junk, ssq[par]], accum=ssq[par].t[:, 0:1])
                      ACT(t1.t[:, 2:3], ssq[par].t[:, 0:1], AF.Ln, [ssq[par]], [t1], scale=1.0 / 256, bias=1e-6)
                      ACT(t1.t[:, 2:3], t1.t[:, 2:3], AF.Exp, [t1], [t1], scale=-0.5)
                      STT("dve", ya.t[:, h * 256:(h + 1) * 256], hb[par].t, t1.t[:, 2:3], sigo.t[:, h * 256:(h + 1) * 256],
                          ALU.mult, ALU.mult, [hb[par], t1, sigo], [ya])
                    b4 = mlp.next()
                    MM(b4.t[:, 0:257], ktok[par].t[:], vt.t[:, ti, h, 0:257], True, True, [ktok[par], vt], [b4])
                    dcol = decbc.t[:, ti * 4 + h:ti * 4 + h + 1]
                    ACT(dtmp[par].t, b4.t[:, 0:257], AF.Copy, [b4, decbc], [dtmp[par]], scale=dcol)
                    STT("dve", stateF.t[:, h, :], stateF.t[:, h, :], dcol, dtmp[par].t, ALU.mult, ALU.add, [state_b[h], decbc, dtmp[par]], [state_b[h]])
                    CP("pool", Cbf.t[:, h, 0:257], stateF.t[:, h, :], [state_b[h]], [cbf_b[h]])
                if dbg and ti == 1:
                    DMA("pool", dya_d[g * 128:(g + 1) * 128, :], ya.t[:, :], [ya], [], ya.bs[0])

        def A_part(g):
            tiles = [g * TPG + ti for ti in range(TPG)]
            gbr = gbrs[g % 2]
            def rhs_q(pb, c0):
                return nqT.t[pb:pb + 64, :, :]

            def bc4(ap):
                return ap.unsqueeze(1).to_broadcast([ap.shape[0], 4, 128])

            def out4(bank):
                return bank.t[:, 0:512].rearrange("p (a b) -> p a b", a=4)

            def fin_cmp(ti, jq, gi):
                sl0 = 128 - 2 * jq
                pb = 64 * gi
                for bi_, bank in enumerate((ocA, ocB)):
                    TS("dve", rs4.t[:, 2 * bi_:2 * bi_ + 2], bank.t[:, 64:512:256], 1e-30, None, ALU.max, None, [bank], [rs4])
                P.op("dve", lambda e: e.reciprocal(out=rs4.t[:, :], in_=rs4.t[:, :]), [rs4], [rs4])
                TS("dve", impb.t, ocA.t[:, 65:193], rs4.t[:, 0:1], None, ALU.mult, None, [ocA, rs4], [impb])
                STT("dve", impb.t, ocA.t[:, 256 + 65:256 + 193], rs4.t[:, 1:2], impb.t, ALU.mult, ALU.add, [ocA, rs4, impb], [impb])
                STT("dve", impb.t, ocB.t[:, 65:193], rs4.t[:, 2:3], impb.t, ALU.mult, ALU.add, [ocB, rs4, impb], [impb])
                STT("dve", impb.t, ocB.t[:, 256 + 65:256 + 193], rs4.t[:, 3:4], impb.t, ALU.mult, ALU.add, [ocB, rs4, impb], [impb])
                TT("dve", w4.t[:, :], rs4.t[:, :], gbr.t[:, ti, gi * 12 + 0:gi * 12 + 12:3], ALU.mult, [rs4, gbr], [w4])
                for bi_, bank in enumerate((ocA, ocB)):
                    TT("dve", yb.t[:, gi * 256 + bi_ * 128:gi * 256 + (bi_ + 1) * 128].rearrange("p (a b) -> p a b", a=2),
                       bank.t[:, 0:512].rearrange("p (a b) -> p a b", a=2)[:, :, 0:64],
                       w4.t[:, 2 * bi_:2 * bi_ + 2].unsqueeze(2).to_broadcast([128, 2, 64]), ALU.mult, [bank, w4], [yb])
                TT("dve", scb.t, impb.t, cf.t[:, F_PMUL + sl0:F_PMUL + sl0 + 128], ALU.mult, [impb, cf], [scb])
                TT("dve", scb.t, scb.t, cf.t[:, F_PADD + sl0:F_PADD + sl0 + 128], ALU.add, [scb, cf], [scb])
                TT("dve", scb.t, scb.t, pcs.t[:, 4:132], ALU.max, [scb, pcs], [scb])
                TT("dve", scb.t, scb.t, pcs.t[:, 132:260], ALU.min, [scb, pcs], [scb])
                P.op("dve", lambda e: e.max(out=m8.t[:, 0:8], in_=scb.t), [scb], [m8])
                P.op("dve", lambda e: e.match_replace(out=sc2b.t, in_to_replace=m8.t[:, 0:8], in_values=scb.t, imm_value=-3e9), [scb, m8], [sc2b])
                P.op("dve", lambda e: e.max(out=m8.t[:, 8:16], in_=sc2b.t), [sc2b], [m8])
                TS("dve", selb.t, scb.t, m8.t[:, 15:16], None, ALU.is_ge, None, [scb, m8], [selb])
                TT("dve", selb.t, selb.t, cf.t[:, F_PELIG + sl0:F_PELIG + sl0 + 128], ALU.mult, [selb, cf], [selb])
                TS("dve", selbb.t[:, :], selb.t, BIG, -BIG, ALU.mult, ALU.add, [selb], [selbb])
                bt = atp.next()
                TR(bfv(bt)[pb:pb + 64, 0:128], selbb.t[:, 0:64], identb, [selbb, cb], [bt])
                TR(bfv(bt)[pb:pb + 64, 128:256], selbb.t[:, 64:128], identb, [selbb, cb], [bt])
                CP("dve", negselT[gi].t[pb:pb + 64, :], bfv(bt)[pb:pb + 64, 0:256], [bt], [negselT[gi]])

            def fin_att(ti, jq, gi, br, bank):
                TS("dve", rs4.t[:, :], bank.t[:, 64:512:128], 1e-30, None, ALU.max, None, [bank], [rs4])
                P.op("dve", lambda e: e.reciprocal(out=rs4.t[:, :], in_=rs4.t[:, :]), [rs4], [rs4])
                TT("dve", w4.t[:, :], rs4.t[:, :], gbr.t[:, ti, gi * 12 + br:gi * 12 + 12:3], ALU.mult, [rs4, gbr], [w4])
                TT("dve", ybtmp.t[:, :].rearrange("p (a b) -> p a b", a=4),
                   bank.t[:, 0:512].rearrange("p (a b) -> p a b", a=4)[:, :, 0:64],
                   w4.t[:, :].unsqueeze(2).to_broadcast([128, 4, 64]), ALU.mult, [bank, w4], [ybtmp])
                TT("dve", yb.t[:, gi * 256:(gi + 1) * 256], yb.t[:, gi * 256:(gi + 1) * 256], ybtmp.t[:, :], ALU.add, [yb, ybtmp], [yb])

            def cmp_item(ti, jq, gi, ct):
                c0 = ti * 128
                pb = 64 * gi
                ctm = jq // 16
                mpat = jq % 16
                stt = {}

                def A():
                    bs_ = atp.next()
                    masked = (ct == ctm) or (ct == ctm - 1 and mpat == 0)
                    MM(out4(bs_), kcT.t[pb:pb + 64, ct * 128:(ct + 1) * 128], rhs_q(pb, c0), True, not masked, [kcT, nqT], [bs_])
                    if ct == ctm:
                        MM(out4(bs_), identb, bc4(cb.t[:, B_MC + mpat * 128:B_MC + (mpat + 1) * 128]), False, True, [cb], [bs_])
                    elif masked:
                        MM(out4(bs_), identb, bc4(cb.t[:, B_MC + 16 * 128:B_MC + 17 * 128]), False, True, [cb], [bs_])
                    pt = ptp.next()
                    if ct == 0:
                        ACT(pt.t, bs_.t[:, 0:512], AF.Exp, [bs_, pcs], [pt], scale=0.125, bias=pcs.t[:, 3:4])
                    else:
                        ACT(pt.t, bs_.t[:, 0:512], AF.Exp, [bs_], [pt], scale=0.125)
                    stt["pt"] = pt

                def B():
                    pt = stt["pt"]
                    for hh in range(4):
                        bank = ocA if hh < 2 else ocB
                        o0 = (hh % 2) * 256
                        MM(bank.t[:, o0:o0 + 193], pt.t[:, hh * 128:(hh + 1) * 128], vcx.t[:, ct, gi, :],
                           ct == 0 and hh % 2 == 0, ct == ctm and hh % 2 == 1, [pt, vcx], [bank], skip=True)
                    if ct == ctm:
                        fin_cmp(ti, jq, gi)
                return (A, B)

            def att_pair(ti, jq, br, kt, kts):
                c0 = ti * 128
                accs = (ocA, ocB)
                stt = {}

                def A():
                    bsl = [atp.next(), atp.next()]
                    if br == 2:
                        slot = kt % 8
                        masked = (kt == jq) or (kt == jq - 4)
                        for gi in range(2):
                            pb = 64 * gi
                            MM(out4(bsl[gi]), kwinT.t[pb:pb + 64, slot * 128:(slot + 1) * 128], rhs_q(pb, c0), True, not masked, [kwin_b[slot], nqT], [bsl[gi]])
                        if masked:
                            mcol = B_TN if kt == jq else B_TN2
                            for gi in range(2):
                                MM(out4(bsl[gi]), identb, bc4(cb.t[:, mcol:mcol + 128]), False, True, [cb], [bsl[gi]])
                    else:
                        q64 = kt // 32
                        for gi in range(2):
                            pb = 64 * gi
                            MM(out4(bsl[gi]), kslcT.t[pb:pb + 64, kt * 128:(kt + 1) * 128], rhs_q(pb, c0), True, False, [kslc_b[kt], nqT], [bsl[gi]])
                        for gi in range(2):
                            pb = 64 * gi
                            MM(out4(bsl[gi]), cb.t[pb:pb + 64, B_W + (kt % 32) * 128:B_W + (kt % 32 + 1) * 128],
                               bc4(negselT[gi].t[pb:pb + 64, q64 * 128:(q64 + 1) * 128]), False, kt != jq, [cb, negselT[gi]], [bsl[gi]])
                        if kt == jq:
                            for gi in range(2):
                                MM(out4(bsl[gi]), identb, bc4(cb.t[:, B_TN:B_TN + 128]), False, True, [cb], [bsl[gi]])
                    pts = []
                    for gi in range(2):
                        pt = ptp.next()
                        if kt == 0:
                            ACT(pt.t, bsl[gi].t[:, 0:512], AF.Exp, [bsl[gi], pcs], [pt], scale=0.125, bias=pcs.t[:, 2:3])
                        else:
                            ACT(pt.t, bsl[gi].t[:, 0:512], AF.Exp, [bsl[gi]], [pt], scale=0.125)
                        pts.append(pt)
                    stt["pts"] = pts

                def B():
                    for gi in range(2):
                        pt = stt["pts"][gi]
                        bank = accs[gi]
                        if br == 2:
                            slot = kt % 8
                            vrhs = vwin.t[:, slot, gi, :]
                            vdep = vwin_b[slot]
                        else:
                            vrhs = vslc.t[:, kt, gi, :]
                            vdep = vslc_b[kt]
                        for hh in range(4):
                            MM(bank.t[:, hh * 128:hh * 128 + 65], pt.t[:, hh * 128:(hh + 1) * 128], vrhs,
                               kt == kts[0] and hh == 0, kt == kts[-1] and hh == 3, [pt, vdep], [bank], skip=True)
                    if kt == kts[-1]:
                        for gi in range(2):
                            fin_att(ti, jq, gi, br, accs[gi])
                return (A, B)

            for ti in (1,):
                jq = tiles[ti]
                items = []
                for gi in range(2):
                    for ct in range(jq // 16 + 1):
                        items.append(cmp_item(ti, jq, gi, ct))
                kts = list(range(max(0, jq - 4), jq + 1))
                for kt in kts:
                    items.append(att_pair(ti, jq, 2, kt, kts))
                kts = list(range(0, jq + 1))
                for kt in kts:
                    items.append(att_pair(ti, jq, 1, kt, kts))
                items[0][0]()
                for i in range(len(items)):
                    if i + 1 < len(items):
                        items[i + 1][0]()
                    items[i][1]()
                if dbg:
                    DMA("pool", dyb_d[g * 128:(g + 1) * 128, :], yb.t[:, :], [yb], [], yb.bs[0])

        def pools(merged):
            if merged:
                mmp.items = [banks[0], banks[1]]
                mlp.items = [banks[0], banks[1]]
            else:
                mmp.items = [banks[0], banks[1], banks[4], banks[5]]
                mlp.items = [banks[6], banks[7], banks[2], banks[3]]

        pools(False)
        B_part(0)
        nq_part(0)
        for g in range(NG):
            pools(True)
            if g % OB != OB - 1:
                ytr_a(g)
                la = P.capture(lambda: A_part(g))
                lb = P.capture(lambda: B_part(g + 1))
                P.replay(Prog.merge(la, lb))
                nq_part(g + 1)
                ytr_b(g)
            else:
                A_part(g)
                ytr_a(g)
                ytr_b(g)
                dense_tail(g - OB + 1)
                if g + 1 < NG:
                    pools(False)
                    B_part(g + 1)
                    nq_part(g + 1)
        P.fence("pool", [ot, ya, yb])
        P.fence("sp", [ot])
        print("ops:", {e: len(P.ops[e]) for e in ENGS}, "sems:", len(P.sems), "sbuf left:", nc.sbuf_bytes_remaining)
        P.emit()
    return nc


_CACHE = {}


def prep_inputs(inp, S):
    perm = make_perm()
    w_in = np.ascontiguousarray(inp["w_in"][0][:, perm])
    b = inp["b_in"][0][perm]
    pv = np.zeros((128, 128), np.float32)
    pv[R_BCM:R_BCM + 32] = b[0:4096].reshape(32, 128)
    pv[R_G1:R_G1 + 8] = inp["norm1_g"][0].reshape(8, 128)
    pv[R_G2:R_G2 + 8] = inp["norm2_g"][0].reshape(8, 128)
    pv[R_GF:R_GF + 8] = inp["norm_f_g"].reshape(8, 128)
    pv[R_CW:R_CW + 32] = inp["conv_w"][0].reshape(4, 8, 128).reshape(32, 128)
    pv[R_CB:R_CB + 8] = inp["conv_b"][0].reshape(8, 128)
    pv[R_BI, 0:4] = b[C_I:C_I + 4]
    pv[R_BF, 0:4] = b[C_F:C_F + 4]
    pv[R_FB, 0:4] = inp["f_bias"][0]
    cf, cb, mm = make_consts(S)
    com = {
        "w_in": w_in, "pv": pv, "brow": np.ascontiguousarray(b[4096:6424].reshape(1, 2328)),
        "gm": np.ascontiguousarray(inp["mlstm_norm_g"][0].reshape(1, 1024)),
        "ckw1": inp["cmp_k_w1"][0], "ckw2": inp["cmp_k_w2"][0], "cvw1": inp["cmp_v_w1"][0], "cvw2": inp["cmp_v_w2"][0],
        "ckposT": np.ascontiguousarray(inp["cmp_k_pos"][0].T), "cvposT": np.ascontiguousarray(inp["cmp_v_pos"][0].T),
        "wa": inp["w_branch_a"][0], "wb": inp["w_branch_b"][0], "wo": inp["w_out"][0],
        "wg": inp["w_ffn_gate"][0], "wu": inp["w_ffn_up"][0], "wd": inp["w_ffn_down"][0],
        "constf": cf, "constb": cb, "mmap": mm,
    }
    return {k: np.ascontiguousarray(v) for k, v in com.items()}


def percore(t):
    pc = np.zeros((128, 260), np.float32)
    pc[:, 0] = float(t)
    pc[:, 1] = (float(t) - 1.0) * 200.0
    pc[:, 2] = 0.0 if t == 1 else -BIG
    if t == 0:
        pc[0:8, 3] = -BIG
    f0 = np.full((128, 128), -3e9, np.float32)
    f0[:, 0 if t == 1 else 2] = 1e9
    v = np.full((128, 128), 3e9, np.float32)
    if t == 0:
        v[:, 0:2] = -1e9
    pc[:, 4:132] = f0
    pc[:, 132:260] = v
    return pc


def local_x(xb, t):
    if t == 1:
        return np.ascontiguousarray(xb)
    return np.ascontiguousarray(np.concatenate([np.zeros((128, xb.shape[1]), xb.dtype), xb[:-128]], axis=0))


def kernel(**inputs):
    inp = {k: np.asarray(v) for k, v in inputs.items()}
    x = inp["x"]
    B, S, _ = x.shape
    if S not in _CACHE:
        _CACHE[S] = build(S)
    nc = _CACHE[S]
    com = prep_inputs(inp, S)
    n = 2 * B
    in_maps = []
    for c in range(n):
        m = dict(com)
        m["x"] = local_x(x[c // 2], c % 2)
        m["pc"] = percore(c % 2)
        in_maps.append(m)
    res = run_bass_kernel_spmd(nc, in_maps, core_ids=list(range(n)))
    out = np.empty((B, S // 128, 128, x.shape[2]), np.float32)
    for c in range(n):
        y = np.asarray(res.results[c]["y"]).reshape(S // 256, 128, x.shape[2])
        out[c // 2, (c % 2)::2] = y
    return out.reshape(B, S, x.shape[2])
```
